# Optimizing a Trainium2 kernel written in Bass

```python
import math
import jax, jax.numpy as jnp
from jax import lax
import numpy as np

D_MODEL = 1024
BATCH = 2
SEQ = 16384
DEPTH = 1
DEC_BATCH = 32
DEC_SEQ = 16
PAST_LEN = 4096

CHUNK = 64
A_HEADS = 8
A_HEAD_DIM = 64
A_WIDTH = A_HEADS * A_HEAD_DIM
A_PREV_CHUNKS = 8
A_WINDOW = (A_PREV_CHUNKS + 1) * CHUNK
A_REL_MAX = 2 * CHUNK
B_HEADS = 4
B_HEAD_DIM = 64
B_VDIM = 2 * B_HEAD_DIM
B_WIDTH = B_HEADS * B_VDIM
M_TOKENS = 256
M_HEADS = 4
M_HEAD_DIM = D_MODEL // M_HEADS
D_FF = 2816
N_BRANCH = 2
IN_COLS = 3 * A_WIDTH + 3 * B_WIDTH
IN_SPLITS = [A_WIDTH, 2 * A_WIDTH, 3 * A_WIDTH, 3 * A_WIDTH + B_WIDTH, 3 * A_WIDTH + 2 * B_WIDTH]
Q_BLOCK = 128
N_NORMS = 9
NORM_EPS = 1e-6
NEG_INF = -1e30

kernel_name = 'hybrid_stream_encoder_step'


def rmsnorm(x, g):
    x32 = x.astype(jnp.float32)
    y = x32 * lax.rsqrt(jnp.mean(x32 * x32, axis=-1, keepdims=True) + NORM_EPS)
    return (y * g.astype(jnp.float32)).astype(x.dtype)


def swiglu(x, w_up, w_down):
    gate, up = jnp.split(x @ w_up, 2, axis=-1)
    return (jax.nn.silu(gate) * up) @ w_down


def alibi_slopes():
    return jnp.asarray(2.0 ** (-8.0 * np.arange(1, B_HEADS + 1) / B_HEADS), dtype=jnp.float32)


def band_attend(q, k, v, pos_q, pos_k, rel_bias):
    cq = pos_q[:, None] // CHUNK
    ck = pos_k[None, :] // CHUNK
    ok = (pos_k[None, :] >= 0) & (ck <= cq) & (ck >= cq - A_PREV_CHUNKS)
    rel = jnp.clip(pos_q[:, None] - pos_k[None, :], -A_REL_MAX, A_REL_MAX) + A_REL_MAX
    bias = rel_bias.astype(jnp.float32)[:, rel]
    s = jnp.einsum('nqhd,nkhd->nhqk', q, k).astype(jnp.float32) * (A_HEAD_DIM ** -0.5) + bias
    p = jax.nn.softmax(jnp.where(ok, s, NEG_INF), axis=-1).astype(v.dtype)
    return jnp.einsum('nhqk,nkhd->nqhd', p, v)


def band_attn_prompt(q, k, v, rel_bias):
    n, s_len = q.shape[:2]
    nc = s_len // CHUNK
    pad = A_PREV_CHUNKS * CHUNK
    kp = jnp.pad(k, ((0, 0), (pad, 0), (0, 0), (0, 0)))
    vp = jnp.pad(v, ((0, 0), (pad, 0), (0, 0), (0, 0)))
    qc = q.reshape(n, nc, CHUNK, A_HEADS, A_HEAD_DIM).swapaxes(0, 1)

    def one_chunk(args):
        c, qi = args
        kb = lax.dynamic_slice_in_dim(kp, c * CHUNK, A_WINDOW, axis=1)
        vb = lax.dynamic_slice_in_dim(vp, c * CHUNK, A_WINDOW, axis=1)
        pos_q = c * CHUNK + jnp.arange(CHUNK, dtype=jnp.int32)
        pos_k = (c - A_PREV_CHUNKS) * CHUNK + jnp.arange(A_WINDOW, dtype=jnp.int32)
        return band_attend(qi, kb, vb, pos_q, pos_k, rel_bias)

    o = lax.map(one_chunk, (jnp.arange(nc, dtype=jnp.int32), qc))
    return o.swapaxes(0, 1).reshape(n, s_len, A_HEADS, A_HEAD_DIM)


def diff_attend(q, k, v, pos_q, pos_k, lam, subln_g, lam_init):
    ok = (pos_k[None, :] // CHUNK) <= (pos_q[:, None] // CHUNK)
    dist = jnp.abs(pos_q[:, None] - pos_k[None, :]).astype(jnp.float32)
    bias = -alibi_slopes()[:, None, None] * dist
    s = jnp.einsum('nqhcd,nkhcd->nhcqk', q, k).astype(jnp.float32) * (B_HEAD_DIM ** -0.5) + bias[:, None]
    p = jax.nn.softmax(jnp.where(ok, s, NEG_INF), axis=-1)
    w = (p[:, :, 0] - lam * p[:, :, 1]).astype(v.dtype)
    o = jnp.einsum('nhqk,nkhe->nqhe', w, v)
    return rmsnorm(o, subln_g) * (1.0 - lam_init)


def diff_attn_prompt(q, k, v, lam, subln_g, lam_init):
    n, s_len = q.shape[:2]
    nb = s_len // Q_BLOCK
    pos_k = jnp.arange(s_len, dtype=jnp.int32)
    qb = q.reshape(n, nb, Q_BLOCK, B_HEADS, 2, B_HEAD_DIM).swapaxes(0, 1)

    def one_block(args):
        i, qi = args
        pos_q = i * Q_BLOCK + jnp.arange(Q_BLOCK, dtype=jnp.int32)
        return diff_attend(qi, k, v, pos_q, pos_k, lam, subln_g, lam_init)

    o = lax.map(one_block, (jnp.arange(nb, dtype=jnp.int32), qb))
    return o.swapaxes(0, 1).reshape(n, s_len, B_HEADS, B_VDIM)


def mem_kv(mem, g, w_mkv):
    n, m, _ = mem.shape
    k, v = jnp.split(rmsnorm(mem, g) @ w_mkv, 2, axis=-1)
    return k.reshape(n, m, M_HEADS, M_HEAD_DIM), v.reshape(n, m, M_HEADS, M_HEAD_DIM)


def mem_attend(xn, mk, mv, w_mq, w_mo):
    n, t, _ = xn.shape
    q = (xn @ w_mq).reshape(n, t, M_HEADS, M_HEAD_DIM)
    s = jnp.einsum('nqhd,nkhd->nhqk', q, mk).astype(jnp.float32) * (M_HEAD_DIM ** -0.5)
    p = jax.nn.softmax(s, axis=-1).astype(mv.dtype)
    o = jnp.einsum('nhqk,nkhd->nqhd', p, mv).reshape(n, t, D_MODEL)
    return o @ w_mo


def pre_mixer(x, g, up1, down1, w_in, w_gate, b_gate):
    h = x + 0.5 * rmsnorm(swiglu(rmsnorm(x, g[0]), up1, down1), g[1])
    u = rmsnorm(h, g[2])
    n, t, _ = u.shape
    qa, ka, va, qb, kb, vb = jnp.split(u @ w_in, IN_SPLITS, axis=-1)
    qa = qa.reshape(n, t, A_HEADS, A_HEAD_DIM)
    ka = ka.reshape(n, t, A_HEADS, A_HEAD_DIM)
    va = va.reshape(n, t, A_HEADS, A_HEAD_DIM)
    qb = qb.reshape(n, t, B_HEADS, 2, B_HEAD_DIM)
    kb = kb.reshape(n, t, B_HEADS, 2, B_HEAD_DIM)
    vb = vb.reshape(n, t, B_HEADS, B_VDIM)
    gates = jax.nn.sigmoid((u @ w_gate + b_gate).astype(jnp.float32)).astype(u.dtype)
    gates = gates.reshape(n, t, N_BRANCH, D_MODEL)
    return h, gates, qa, ka, va, qb, kb, vb


def post_mixer(h, ya, yb, gates, mk, mv, g, w_br_a, w_br_b, w_out, w_mq, w_mo, up2, down2):
    n, t, _ = h.shape
    merged = (gates[:, :, 0] * (ya.reshape(n, t, A_WIDTH) @ w_br_a)
              + gates[:, :, 1] * (yb.reshape(n, t, B_WIDTH) @ w_br_b))
    h = h + rmsnorm(merged @ w_out, g[3])
    h = h + rmsnorm(mem_attend(rmsnorm(h, g[4]), mk, mv, w_mq, w_mo), g[5])
    return h + 0.5 * rmsnorm(swiglu(rmsnorm(h, g[7]), up2, down2), g[8])


def _normal(k, shape, scale):
    return jax.random.normal(k, shape, jnp.float32) * scale


def setup_inputs(seed: int = 0) -> dict:
    key = jax.random.key(seed)
    ks = jax.random.split(key, 32)
    la = min(A_WINDOW, PAST_LEN)
    d_in = D_MODEL ** -0.5
    return {
        'x_prompt': _normal(ks[0], (BATCH, SEQ, D_MODEL), 1.0),
        'x_sample': _normal(ks[1], (DEC_BATCH, DEC_SEQ, D_MODEL), 1.0),
        'cache_a_k': _normal(ks[2], (DEPTH, DEC_BATCH, la, A_HEADS, A_HEAD_DIM), 1.0),
        'cache_a_v': _normal(ks[3], (DEPTH, DEC_BATCH, la, A_HEADS, A_HEAD_DIM), 1.0),
        'cache_b_k': _normal(ks[4], (DEPTH, DEC_BATCH, PAST_LEN, B_HEADS, 2, B_HEAD_DIM), 1.0),
        'cache_b_v': _normal(ks[5], (DEPTH, DEC_BATCH, PAST_LEN, B_HEADS, B_VDIM), 1.0),
        'cache_mem_k': _normal(ks[6], (DEPTH, DEC_BATCH, M_TOKENS, M_HEADS, M_HEAD_DIM), 1.0),
        'cache_mem_v': _normal(ks[7], (DEPTH, DEC_BATCH, M_TOKENS, M_HEADS, M_HEAD_DIM), 1.0),
        'mem_prompt': _normal(ks[8], (BATCH, M_TOKENS, D_MODEL), 1.0),
        'w_in': _normal(ks[9], (DEPTH, D_MODEL, IN_COLS), d_in),
        'w_gate': _normal(ks[10], (DEPTH, D_MODEL, N_BRANCH * D_MODEL), d_in),
        'b_gate': _normal(ks[11], (DEPTH, N_BRANCH * D_MODEL), 0.1),
        'rel_bias': _normal(ks[12], (DEPTH, A_HEADS, 2 * A_REL_MAX + 1), 0.5),
        'lam_qk': _normal(ks[13], (DEPTH, 4, B_HEAD_DIM), 0.1),
        'subln_g': 1.0 + _normal(ks[14], (DEPTH, B_VDIM), 0.05),
        'w_br_a': _normal(ks[15], (DEPTH, A_WIDTH, D_MODEL), A_WIDTH ** -0.5),
        'w_br_b': _normal(ks[16], (DEPTH, B_WIDTH, D_MODEL), B_WIDTH ** -0.5),
        'w_out': _normal(ks[17], (DEPTH, D_MODEL, D_MODEL), d_in),
        'w_mq': _normal(ks[18], (DEPTH, D_MODEL, D_MODEL), d_in),
        'w_mkv': _normal(ks[19], (DEPTH, D_MODEL, 2 * D_MODEL), d_in),
        'w_mo': _normal(ks[20], (DEPTH, D_MODEL, D_MODEL), d_in),
        'norm_g': 1.0 + _normal(ks[21], (DEPTH, N_NORMS, D_MODEL), 0.05),
        'ffn1_up': _normal(ks[22], (DEPTH, D_MODEL, 2 * D_FF), d_in),
        'ffn1_down': _normal(ks[23], (DEPTH, D_FF, D_MODEL), D_FF ** -0.5),
        'ffn2_up': _normal(ks[24], (DEPTH, D_MODEL, 2 * D_FF), d_in),
        'ffn2_down': _normal(ks[25], (DEPTH, D_FF, D_MODEL), D_FF ** -0.5),
    }


def reference(x_prompt, x_sample, cache_a_k, cache_a_v, cache_b_k, cache_b_v, cache_mem_k, cache_mem_v,
              mem_prompt, w_in, w_gate, b_gate, rel_bias, lam_qk, subln_g, w_br_a, w_br_b, w_out,
              w_mq, w_mkv, w_mo, norm_g, ffn1_up, ffn1_down, ffn2_up, ffn2_down):
    s_len = x_prompt.shape[1]
    t_len = x_sample.shape[1]
    past = cache_b_k.shape[2]
    l_a = cache_a_k.shape[2]
    keep_a = min(A_WINDOW, s_len)
    pos_q_s = past + jnp.arange(t_len, dtype=jnp.int32)
    pos_k_sa = jnp.concatenate([past - l_a + jnp.arange(l_a, dtype=jnp.int32), pos_q_s])
    pos_k_sb = jnp.arange(past + t_len, dtype=jnp.int32)

    xp, xs = x_prompt, x_sample
    ak_p, av_p, bk_p, bv_p, mk_p, mv_p = [], [], [], [], [], []
    ak_s, av_s, bk_s, bv_s = [], [], [], []
    for l in range(DEPTH):
        g = norm_g[l]
        lam_init = 0.8 - 0.6 * math.exp(-0.3 * l)
        lq = lam_qk[l].astype(jnp.float32)
        lam = jnp.exp(jnp.sum(lq[0] * lq[1])) - jnp.exp(jnp.sum(lq[2] * lq[3])) + lam_init

        h, gates, qa, ka, va, qb, kb, vb = pre_mixer(xp, g, ffn1_up[l], ffn1_down[l], w_in[l], w_gate[l], b_gate[l])
        ya = band_attn_prompt(qa, ka, va, rel_bias[l])
        yb = diff_attn_prompt(qb, kb, vb, lam, subln_g[l], lam_init)
        mk, mv = mem_kv(mem_prompt, g[6], w_mkv[l])
        xp = post_mixer(h, ya, yb, gates, mk, mv, g, w_br_a[l], w_br_b[l], w_out[l], w_mq[l], w_mo[l],
                        ffn2_up[l], ffn2_down[l])
        ak_p.append(ka[:, s_len - keep_a:])
        av_p.append(va[:, s_len - keep_a:])
        bk_p.append(kb)
        bv_p.append(vb)
        mk_p.append(mk)
        mv_p.append(mv)

        h, gates, qa, ka, va, qb, kb, vb = pre_mixer(xs, g, ffn1_up[l], ffn1_down[l], w_in[l], w_gate[l], b_gate[l])
        ka_all = jnp.concatenate([cache_a_k[l], ka], axis=1)
        va_all = jnp.concatenate([cache_a_v[l], va], axis=1)
        ya = band_attend(qa, ka_all, va_all, pos_q_s, pos_k_sa, rel_bias[l])
        kb_all = jnp.concatenate([cache_b_k[l], kb], axis=1)
        vb_all = jnp.concatenate([cache_b_v[l], vb], axis=1)
        yb = diff_attend(qb, kb_all, vb_all, pos_q_s, pos_k_sb, lam, subln_g[l], lam_init)
        xs = post_mixer(h, ya, yb, gates, cache_mem_k[l], cache_mem_v[l], g, w_br_a[l], w_br_b[l], w_out[l],
                        w_mq[l], w_mo[l], ffn2_up[l], ffn2_down[l])
        ak_s.append(ka_all[:, ka_all.shape[1] - l_a:])
        av_s.append(va_all[:, va_all.shape[1] - l_a:])
        bk_s.append(kb)
        bv_s.append(vb)

    return (xp, xs,
            jnp.stack(ak_p), jnp.stack(av_p), jnp.stack(bk_p), jnp.stack(bv_p), jnp.stack(mk_p), jnp.stack(mv_p),
            jnp.stack(ak_s), jnp.stack(av_s), jnp.stack(bk_s), jnp.stack(bv_s))
```

```python
import os
import numpy as np
import ml_dtypes
from contextlib import ExitStack
import concourse.bass as bass
import concourse.mybir as mybir
from concourse.bass_utils import run_bass_kernel_spmd

F32 = mybir.dt.float32
BF16 = mybir.dt.bfloat16
AF = mybir.ActivationFunctionType
ALU = mybir.AluOpType
AX = mybir.AxisListType

ENGS = ["pe", "act", "dve", "pool", "sp"]
KDMA = 8


class Tl:
    __slots__ = ("ap", "lw", "rd", "rd_dma", "name")

    def __init__(self, ap, name=""):
        self.ap = ap
        self.lw = None
        self.rd = {}
        self.rd_dma = []
        self.name = name

    def __getitem__(self, k):
        return self.ap[k]


class Op:
    __slots__ = ("eng", "fn", "waits", "need_sig", "dma", "semkey", "semval", "cc")

    def __init__(self, eng, fn, dma=False, cc=False):
        self.eng = eng
        self.fn = fn
        self.waits = []
        self.need_sig = False
        self.dma = dma
        self.cc = cc
        self.semkey = None
        self.semval = None


class Prog:
    def __init__(self):
        self.q = {e: [] for e in ENGS}
        self.out_dmas = []

    def add(self, eng, fn, reads=(), writes=(), dma=False, cc=False, out=False):
        op = Op(eng, fn, dma, cc)
        deps = []
        for t in reads:
            if t.lw is not None:
                deps.append((t.lw, "raw"))
            if t.name.startswith("pf") or t.name.startswith("pb"):
                for e2, r in t.rd.items():
                    if e2 != eng:
                        deps.append((r, "rar"))
        for t in writes:
            if t.lw is not None:
                deps.append((t.lw, "waw"))
            for r in t.rd.values():
                deps.append((r, "war"))
            for r in t.rd_dma:
                deps.append((r, "war"))
        async_op = dma or cc
        seen = set()
        for d, kind in deps:
            if d is op or id(d) in seen:
                continue
            d_async = d.dma or d.cc
            if (not d_async) and (not async_op) and d.eng == eng:
                if eng == "pe":
                    continue
            seen.add(id(d))
            d.need_sig = True
            op.waits.append(d)
        for t in reads:
            if async_op:
                t.rd_dma.append(op)
                if len(t.rd_dma) > 24:
                    t.rd_dma = t.rd_dma[-24:]
            else:
                t.rd[eng] = op
        for t in writes:
            t.lw = op
            t.rd = {}
            t.rd_dma = []
        self.q[eng].append(op)
        if out:
            self.out_dmas.append(op)
        return op

    def barrier(self, mk, extra={}):
        marks = []
        for e in ENGS:
            t, fn = mk(e, 0)
            pend = [o for o in self.q[e] if (o.dma or o.cc)]
            op = self.add(e, fn, reads=(), writes=(t,) + tuple(extra.get(e, ())), dma=(e == "sp"))
            for o in pend[-(KDMA + 2):]:
                if o not in op.waits:
                    o.need_sig = True
                    op.waits.append(o)
            marks.append(t)
        for e in ENGS:
            t, fn = mk(e, 1)
            self.add(e, fn, reads=marks, writes=(t,) + tuple(extra.get(e, ())), dma=(e == "sp"))

    def emit(self, nc, stack):
        sems = {}
        for e in ENGS:
            sems[e] = stack.enter_context(nc.semaphore("s_" + e))
            sems[("cc", e)] = stack.enter_context(nc.semaphore("c_" + e))
            for j in range(KDMA):
                sems[(e, j)] = stack.enter_context(nc.semaphore("d_%s%d" % (e, j)))
        for e in ENGS:
            cnt = 0
            nd = 0
            ncc = 0
            dmas = []
            for op in self.q[e]:
                if op.cc:
                    ncc += 1
                    op.semkey = ("cc", e)
                    op.semval = ncc
                elif op.dma:
                    op.semkey = (e, nd % KDMA)
                    op.semval = 16 * (nd // KDMA + 1)
                    if nd >= KDMA:
                        op.waits.append(dmas[nd - KDMA])
                    dmas.append(op)
                    nd += 1
                elif op.need_sig:
                    cnt += 1
                    op.semkey = e
                    op.semval = cnt
        block = stack.enter_context(nc.Block())
        prog = self

        def body(e):
            def run(engine):
                known = {}
                for op in prog.q[e]:
                    for d in op.waits:
                        k, v = d.semkey, d.semval
                        if known.get(k, 0) < v:
                            engine.wait_ge(sems[k], v)
                            known[k] = v
                    ins = op.fn(engine)
                    if op.cc:
                        ins.then_inc(sems[op.semkey], 1)
                    elif op.dma:
                        ins.then_inc(sems[op.semkey], 16)
                    elif op.need_sig:
                        ins.then_inc(sems[e], 1)
                if e == "sp":
                    for d in prog.out_dmas:
                        k, v = d.semkey, d.semval
                        if known.get(k, 0) < v:
                            engine.wait_ge(sems[k], v)
                            known[k] = v
            return run

        block.tensor(body("pe"))
        block.scalar(body("act"))
        block.vector(body("dve"))
        block.gpsimd(body("pool"))
        block.sync(body("sp"))


D = 1024
FF = 2816
NPT = 4096
NS = 64
NTOK = NPT + NS
EPS = 1e-6
SLOPES = [2.0 ** (-8.0 * (h + 1) / 4) for h in range(4)]
NEG = -1.0e30
LAM_INIT = 0.8 - 0.6
STAGE = 3


def _mm(out, lhsT, rhs, st, sp):
    return lambda e: e.matmul(out, lhsT=lhsT, rhs=rhs, start=st, stop=sp)


def _mmx(out, lhsT, rhs, st):
    return lambda e: e.matmul(out, lhsT=lhsT, rhs=rhs, start=st, stop=False, skip_group_check=True)


def _tp(out, in_, ident):
    return lambda e: e.transpose(out=out, in_=in_, identity=ident)


def _dma(out, in_):
    return lambda e: e.dma_start(out=out, in_=in_)


def _act(out, in_, func, bias=None, scale=None, accum=None):
    kw = {}
    if bias is not None:
        kw["bias"] = bias
    if scale is not None:
        kw["scale"] = scale
    if accum is not None:
        kw["accum_out"] = accum
    return lambda e: e.activation(out=out, in_=in_, func=func, **kw)


def _copy(out, in_):
    return lambda e: e.tensor_copy(out=out, in_=in_)


def _ts(out, in0, s1, s2, op0, op1=None):
    if op1 is None:
        return lambda e: e.tensor_scalar(out=out, in0=in0, scalar1=s1, scalar2=None, op0=op0)
    return lambda e: e.tensor_scalar(out=out, in0=in0, scalar1=s1, scalar2=s2, op0=op0, op1=op1)


def _stt(out, in0, scalar, in1, op0, op1):
    return lambda e: e.scalar_tensor_tensor(out=out, in0=in0, scalar=scalar, in1=in1, op0=op0, op1=op1)


def _tt(out, in0, in1, op):
    return lambda e: e.tensor_tensor(out=out, in0=in0, in1=in1, op=op)


def _memset(ap, v):
    return lambda e: e.memset(ap, v)


class K:
    def __init__(self):
        self.nc = bass.Bass("TRN2", target_bir_lowering=False)
        self.P = Prog()
        self.dram = {}

    def din(self, name, shape, dt=F32):
        t = self.nc.dram_tensor(name, list(shape), dt, kind="ExternalInput")
        self.dram[name] = t
        return t.ap()

    def dout(self, name, shape, dt=F32):
        t = self.nc.dram_tensor(name, list(shape), dt, kind="ExternalOutput")
        self.dram[name] = t
        return t.ap()

    def dint(self, name, shape, dt):
        t = self.nc.dram_tensor(name, list(shape), dt)
        self.dram[name] = t
        return t.ap()

    def sb(self, st, name, shape, dt):
        return Tl(st.enter_context(self.nc.sbuf_tensor(name, list(shape), dt)), name)

    def rms_T(self, src, rows, TB, gain, dstT, col0=0):
        P = self.P
        ss = self.ss
        P.add("dve", _memset(ss[:rows, 0:TB], 0.0), writes=[ss])
        for tb in range(TB):
            P.add("act", _act(self.junk[:rows, :], src[:rows, tb, :], AF.Square, accum=ss[:rows, tb:tb + 1]),
                  reads=[src, ss], writes=[self.junk, ss])
        self.rstd(ss, rows, TB, D * EPS)
        for tb in range(TB):
            xn = self.xn[tb % 2]
            P.add("dve", _stt(xn[:rows, :], src[:rows, tb, :], self.rs[:rows, tb:tb + 1], gain[:rows, :],
                              ALU.mult, ALU.mult), reads=[src, self.rs, gain], writes=[xn])
            pb = self.pb[tb % 2]
            for kc in range(8):
                P.add("pe", _tp(pb[:, kc * 128:kc * 128 + rows], xn[:rows, kc * 128:(kc + 1) * 128],
                                self.identb[:rows, :rows]), reads=[xn, self.identb], writes=[pb])
            c0 = col0 + tb * rows
            eng = "act" if tb % 2 == 0 else "dve"
            src_ap = pb[:, 0:1024].rearrange("p (k c) -> p k c", k=8)[:, :, 0:rows]
            if eng == "act":
                P.add("act", _act(dstT[:, :, c0:c0 + rows], src_ap, AF.Copy), reads=[pb], writes=[dstT])
            else:
                P.add("dve", _copy(dstT[:, :, c0:c0 + rows], src_ap), reads=[pb], writes=[dstT])


    def rstd(self, ss, rows, n, eps_tot):
        P = self.P
        P.add("dve", _ts(self.rs[:rows, 0:n], ss[:rows, 0:n], float(eps_tot), None, ALU.add), reads=[ss], writes=[self.rs])
        P.add("act", _act(self.rs[:rows, 0:n], self.rs[:rows, 0:n], AF.Sqrt), reads=[self.rs], writes=[self.rs])
        P.add("dve", lambda e: e.reciprocal(out=self.rs[:rows, 0:n], in_=self.rs[:rows, 0:n]), reads=[self.rs], writes=[self.rs])

    def load_w(self, wb, view, src, k):
        tls = self.wtl.get(src.tensor.name, [])
        self.P.add("pool", _dma(view, src.rearrange("(k p) c -> p k c", p=128)), reads=tls, writes=[wb], dma=True)

    def convert_w(self, name):
        f32, bf = self.wf32[name]
        R, C = f32.shape
        tls = self.wtl[bf.tensor.name]
        for r0 in range(0, R, 128):
            for c0 in range(0, C, 2048):
                c1 = min(C, c0 + 2048)
                t = Tl(None, "wc")
                tls.append(t)
                self.P.add("pool", _dma(bf[r0:r0 + 128, c0:c1], f32[r0:r0 + 128, c0:c1]), writes=[t], dma=True)

    def swiglu_ffn(self, xT, ntok, rows, TB, w_up, w_down, y_t):
        P = self.P
        aT = self.aT
        wi = 0
        pair = 0
        for g in range(4):
            npair = 6 if g < 3 else 4
            wb = self.wbuf[self.wcnt % 2]
            self.wcnt += 1
            wv = wb[:, 0:12288].rearrange("p (k c) -> p k c", k=8)
            c0 = g * 768
            nc_ = npair * 128
            self.load_w(wb, wv[:, :, 0:nc_], w_up[:, c0:c0 + nc_], 8)
            self.load_w(wb, wv[:, :, 768:768 + nc_], w_up[:, FF + c0:FF + c0 + nc_], 8)
            for pi in range(npair):
                pg = self.pf[(pair % 2) * 2]
                pu = self.pf[(pair % 2) * 2 + 1]
                for kc in range(8):
                    P.add("pe", _mm(pg[:, 0:ntok], wv[:, kc, pi * 128:(pi + 1) * 128], xT[:, kc, 0:ntok], kc == 0, kc == 7),
                          reads=[wb, xT], writes=[pg])
                for kc in range(8):
                    P.add("pe", _mm(pu[:, 0:ntok], wv[:, kc, 768 + pi * 128:768 + (pi + 1) * 128], xT[:, kc, 0:ntok],
                                    kc == 0, kc == 7), reads=[wb, xT], writes=[pu])
                sg = self.sg[pair % 2]
                P.add("act", _act(sg[:, 0:ntok], pg[:, 0:ntok], AF.Silu), reads=[pg], writes=[sg])
                P.add("dve", _tt(aT[:, pair, 0:ntok], sg[:, 0:ntok], pu[:, 0:ntok], ALU.mult),
                      reads=[sg, pu], writes=[aT])
                pair += 1
        for half in range(2):
            wb = self.wbuf[self.wcnt % 2]
            self.wcnt += 1
            wv = wb[:, 0:11264].rearrange("p (k c) -> p k c", k=22)
            self.load_w(wb, wv, w_down[:, half * 512:(half + 1) * 512], 22)
            for tb in range(TB):
                ps = self.pf[4 + (tb % 2)]
                for fc in range(22):
                    P.add("pe", _mm(ps[:rows, :], aT[:, fc, tb * rows:(tb + 1) * rows], wv[:, fc, :], fc == 0, fc == 21),
                          reads=[aT, wb], writes=[ps])
                P.add("act", _act(y_t[:rows, tb, half * 512:(half + 1) * 512], ps[:rows, :], AF.Copy),
                      reads=[ps], writes=[y_t])

    def resid_norm(self, h_t, y_t, rows, TB, gain, scale_in_gain=True):
        P = self.P
        ss = self.ss
        P.add("dve", _memset(ss[:rows, 0:TB], 0.0), writes=[ss])
        for tb in range(TB):
            P.add("act", _act(self.junk[:rows, :], y_t[:rows, tb, :], AF.Square, accum=ss[:rows, tb:tb + 1]),
                  reads=[y_t, ss], writes=[self.junk, ss])
        self.rstd(ss, rows, TB, D * EPS)
        for tb in range(TB):
            P.add("dve", _stt(y_t[:rows, tb, :], y_t[:rows, tb, :], self.rs[:rows, tb:tb + 1], gain[:rows, :],
                              ALU.mult, ALU.mult), reads=[y_t, self.rs, gain], writes=[y_t])
            P.add("dve", _tt(h_t[:rows, tb, :], h_t[:rows, tb, :], y_t[:rows, tb, :], ALU.add),
                  reads=[y_t, h_t], writes=[h_t])

    def load_gain(self, g_tl, idx, coef):
        P = self.P
        src = bass.AP(self.norm_g.tensor, idx * D, [[0, 128], [1, D]])
        P.add("sp", _dma(g_tl[:, :], src), writes=[g_tl], dma=True)
        P.add("dve", _ts(g_tl[:, :], g_tl[:, :], float(coef), None, ALU.mult), reads=[g_tl], writes=[g_tl])


    def mem_kv(self, memp, g6, w_mkv, mk_p, mv_p, x_t, xT, kvst, qst, mkT_scr, t_mkT, mv_scr, t_mv):
        P = self.P
        P.add("sp", _dma(x_t[:, 0:2, :], memp.rearrange("(tb p) d -> p tb d", p=128)), writes=[x_t], dma=True)
        self.rms_T(x_t, 128, 2, g6, xT)
        for grp in range(2):
            wb = self.wbuf[self.wcnt % 2]
            self.wcnt += 1
            wv = wb[:, 0:12288].rearrange("p (k c) -> p k c", k=8)
            self.load_w(wb, wv[:, :, 0:1024], w_mkv[:, grp * 1024:(grp + 1) * 1024], 8)
            for tb in range(2):
                for half in range(2):
                    ps = self.pf[4 + half]
                    for kc in range(8):
                        P.add("pe", _mm(ps[:, :], xT[:, kc, tb * 128:(tb + 1) * 128], wv[:, kc, half * 512:(half + 1) * 512],
                                        kc == 0, kc == 7), reads=[wb, xT], writes=[ps])
                    P.add("dve", _copy(kvst[:, tb * 2 + half, :], ps[:, :]), reads=[ps], writes=[kvst])
            outt = mk_p if grp == 0 else mv_p
            for tb in range(2):
                P.add("sp", _dma(outt[tb * 128:(tb + 1) * 128, :].rearrange("p (hf d) -> p hf d", hf=2),
                                 kvst[:, 2 * tb:2 * tb + 2, :]), reads=[kvst], dma=True, out=True)
            if grp == 1:
                P.add("act", _act(qst[:, :, :], kvst[:, :, :], AF.Copy), reads=[kvst], writes=[qst])
                for tb in range(2):
                    P.add("sp", _dma(mv_scr[tb * 128:(tb + 1) * 128, :].rearrange("p (hf d) -> p hf d", hf=2),
                                     qst[:, 2 * tb:2 * tb + 2, :]), reads=[qst], writes=[t_mv], dma=True)
            else:
                for c in range(8):
                    ps = self.pf[c % 4]
                    for kc in range(8):
                        P.add("pe", _mm(ps[:, 0:256], wv[:, kc, c * 128:(c + 1) * 128], xT[:, kc, 0:256], kc == 0, kc == 7),
                              reads=[wb, xT], writes=[ps])
                    P.add("act", _act(self.junk[:, 0:256], ps[:, 0:256], AF.Copy), reads=[ps], writes=[self.junk])
                    P.add("sp", _dma(mkT_scr[c, :, :], self.junk[:, 0:256]), reads=[self.junk], writes=[t_mkT], dma=True)


    def exchange(self, si, L):
        P = self.P
        RG = [[0, 1, 2, 3], [4, 5, 6, 7]]
        for nm in ["kbT", "vb", "kaT", "va"]:
            a_in, t_in_list = L[nm + "_in"], L["t_" + nm + "_in"][si]
            a_g, t_g = L[nm + "_g"], L["t_" + nm + "_g"][si]
            src = a_in[si * 1024:(si + 1) * 1024, :]
            dst = a_g[si * 4096:(si + 1) * 4096, :]
            if os.environ.get("K_NOAG"):
                for sr in range(4):
                    P.add("sp", _dma(dst[sr * 1024:(sr + 1) * 1024, :], src), reads=t_in_list, writes=[t_g], dma=True)
                continue
            P.add("pool", (lambda a, b: (lambda e: e.collective_compute("AllGather", ALU.bypass, replica_groups=RG,
                                                                         ins=[a.opt()], outs=[b.opt()])))(src, dst),
                  reads=t_in_list, writes=[t_g], cc=True)

    def phase23(self, top, L):
        self.phase2(top, L)
        self.phase3(top, L)

    def phase2(self, top, L):
        P = self.P
        nc = self.nc
        pf, pb = self.pf, self.pb
        identb = self.identb
        kbT_g, vb_g, kaT_g, va_g = L["kbT_g"], L["vb_g"], L["kaT_g"], L["va_g"]
        t_kbT_g, t_vb_g, t_kaT_g, t_va_g = L["t_kbT_g"], L["t_vb_g"], L["t_kaT_g"], L["t_va_g"]
        qa_scr, qb_scr, yT_scr = L["qa_scr"], L["qb_scr"], L["yT_scr"]
        t_qa, t_qb, t_yT = L["t_qa"], L["t_qb"], L["t_yT"]
        p2 = ExitStack()
        with p2:
            s2 = lambda name, shape, dt: self.sb(p2, name, shape, dt)
            btab = s2("btab", [128, 4, 128], F32)
            dtab = s2("dtab", [128, 4, 8, 256], BF16)
            btabs = s2("btabs", [128, 4, 33], F32)
            dtabs = s2("dtabs", [16, 4, 16], BF16)
            maska = s2("maska", [128, 12, 256], F32)
            lamt = s2("lamt", [128, 256], F32)
            lamp = s2("lamp", [128, 128], F32)
            lamv = s2("lamv", [128, 8], F32)
            sg8 = s2("sg8", [128, 128], F32)
            qbt = [[s2("qbt%d_%d" % (i, c), [128, 4, 256], BF16) for c in range(2)] for i in range(2)]
            kt = [s2("kt%d" % i, [128, 2, 256], BF16) for i in range(4)]
            vt = [s2("vt%d" % i, [128, 2, 2, 130], BF16) for i in range(4)]
            pt = [s2("pt%d" % i, [128, 512], BF16) for i in range(4)]
            sbs = [s2("sbs%d" % i, [128, 512], F32) for i in range(3)]
            ep_r = s2("ep_r", [128, 8], F32)
            ep_t = [s2("ep_t%d" % i, [128, 128], F32) for i in range(2)]
            ep_y = [s2("ep_y%d" % i, [128, 128], F32) for i in range(2)]
            yb = [s2("yb%d" % i, [128, 2, 256], BF16) for i in range(2)]
            ybT = [s2("ybT%d" % i, [128, 2, 256], BF16) for i in range(2)]
            gt = s2("gt", [128, 12, 2, 256], F32)
            qat = [[s2("qat%d_%d" % (i, c), [128, 256], BF16) for c in range(2)] for i in range(2)]
            kat = [s2("kat%d" % i, [128, 2, 128], BF16) for i in range(4)]
            vat = [s2("vat%d" % i, [128, 2, 2, 66], BF16) for i in range(4)]
            ya = [s2("ya%d" % i, [128, 2, 128], BF16) for i in range(2)]
            yaT = [s2("yaT%d" % i, [128, 256], BF16) for i in range(2)]

            P.add("sp", _dma(btab[:, :, :], L["c_btab"].rearrange("p (h o) -> p h o", h=4)), writes=[btab], dma=True)
            P.add("sp", _dma(dtab[:, :, :, :], L["c_dtab"].rearrange("p (h i q) -> p h i q", h=4, i=8)), writes=[dtab], dma=True)
            P.add("sp", _dma(btabs[:, :, :], L["c_btabs"].rearrange("p (h o) -> p h o", h=4)), writes=[btabs], dma=True)
            P.add("sp", _dma(dtabs[:, :, :], L["c_dtabs"].rearrange("p (h o) -> p h o", h=4)), writes=[dtabs], dma=True)
            P.add("sp", _dma(maska[:, :, :], L["c_maska"].rearrange("p (i q) -> p i q", i=12)), writes=[maska], dma=True)
            for pair in qbt:
                for t_ in pair:
                    P.add("dve", _memset(t_[:, :, :], 0.0), writes=[t_])
            for pair in qat:
                for t_ in pair:
                    P.add("dve", _memset(t_[:, :], 0.0), writes=[t_])
            for v in vt:
                P.add("dve", _memset(v[:, :, :, 128:130], 1.0), writes=[v])
            for v in vat:
                P.add("dve", _memset(v[:, :, :, 64:66], 1.0), writes=[v])
            vt_b = [[Tl(v.ap, v.name + "_b%d" % bb) for bb in range(2)] for v in vt]
            vat_b = [[Tl(v.ap, v.name + "_b%d" % bb) for bb in range(2)] for v in vat]
            for v, vb2 in list(zip(vt, vt_b)) + list(zip(vat, vat_b)):
                for t_ in vb2:
                    t_.lw = v.lw
            P.add("sp", _dma(lamt[:, :], bass.AP(L["lam_qk"].tensor, 0, [[0, 128], [1, 256]])), writes=[lamt], dma=True)
            P.add("dve", _tt(lamp[:, 0:64], lamt[:, 0:64], lamt[:, 64:128], ALU.mult), reads=[lamt], writes=[lamp])
            P.add("dve", _tt(lamp[:, 64:128], lamt[:, 128:192], lamt[:, 192:256], ALU.mult), reads=[lamt], writes=[lamp])
            P.add("dve", lambda e: e.reduce_sum(out=lamv[:, 0:1], in_=lamp[:, 0:64], axis=AX.X), reads=[lamp], writes=[lamv])
            P.add("dve", lambda e: e.reduce_sum(out=lamv[:, 1:2], in_=lamp[:, 64:128], axis=AX.X), reads=[lamp], writes=[lamv])
            P.add("act", _act(lamv[:, 2:4], lamv[:, 0:2], AF.Exp), reads=[lamv], writes=[lamv])
            P.add("dve", _tt(lamv[:, 4:5], lamv[:, 3:4], lamv[:, 2:3], ALU.subtract), reads=[lamv], writes=[lamv])
            P.add("dve", _ts(lamv[:, 5:6], lamv[:, 4:5], -LAM_INIT, None, ALU.add), reads=[lamv], writes=[lamv])
            nlam = lamv[:, 5:6]
            P.add("sp", _dma(sg8[:, :], bass.AP(L["subln_g"].tensor, 0, [[0, 128], [1, 128]])), writes=[sg8], dma=True)
            P.add("dve", _ts(sg8[:, :], sg8[:, :], float((1.0 - LAM_INIT) * np.sqrt(128.0)), None, ALU.mult), reads=[sg8], writes=[sg8])

            def acc(idx, rows, ncol=130):
                per = 512 // ncol
                t = pf[2 + idx // per]
                c0 = (idx % per) * ncol
                return t, t[:rows, c0:c0 + ncol]

            cnt = {"s": 0, "k": 0, "pt": 0, "ep": 0, "yb": 0, "q": 0, "sb": 0}
            opened = set()

            def mm2(idx, rows, ncol, lhsT, rhs, first, reads):
                ta, aa = acc(idx, rows, ncol)
                st = False
                if first and ta.name not in opened:
                    opened.add(ta.name)
                    st = True
                P.add("pe", _mmx(aa, lhsT, rhs, st), reads=reads, writes=[ta])

            def diff_epilogue(rows, hh_list, n_qs, dst_fn, i0=lambda hh, qs: hh * 4 + qs, i1=lambda hh, qs: hh * 4 + 2 + qs, getacc=None):
                if getacc is None:
                    getacc = acc
                for hh in hh_list:
                    for qs in range(n_qs):
                        t0, a0 = getacc(i0(hh, qs), rows)
                        t1, a1 = getacc(i1(hh, qs), rows)
                        et = ep_t[cnt["ep"] % 2]
                        ey = ep_y[cnt["ep"] % 2]
                        cnt["ep"] += 1
                        P.add("dve", lambda e, a=a0: e.reciprocal(out=ep_r[:rows, 0:1], in_=a[:, 128:129]), reads=[t0], writes=[ep_r])
                        P.add("dve", lambda e, a=a1: e.reciprocal(out=ep_r[:rows, 1:2], in_=a[:, 128:129]), reads=[t1, ep_r], writes=[ep_r])
                        P.add("dve", _tt(ep_r[:rows, 2:3], ep_r[:rows, 1:2], nlam[:rows, :], ALU.mult), reads=[ep_r, lamv], writes=[ep_r])
                        P.add("dve", _ts(et[:rows, :], a0[:, 0:128], ep_r[:rows, 0:1], None, ALU.mult), reads=[t0, ep_r], writes=[et])
                        P.add("dve", _stt(ey[:rows, :], a1[:, 0:128], ep_r[:rows, 2:3], et[:rows, :], ALU.mult, ALU.add),
                              reads=[t1, ep_r, et], writes=[ey])
                        P.add("dve", _memset(ep_r[:rows, 3:4], 0.0), reads=[ep_r], writes=[ep_r])
                        P.add("act", _act(et[:rows, :], ey[:rows, :], AF.Square, accum=ep_r[:rows, 3:4]), reads=[ey, ep_r], writes=[et, ep_r])
                        P.add("dve", _ts(ep_r[:rows, 4:5], ep_r[:rows, 3:4], 128.0 * EPS, None, ALU.add), reads=[ep_r], writes=[ep_r])
                        P.add("act", _act(ep_r[:rows, 4:5], ep_r[:rows, 4:5], AF.Sqrt), reads=[ep_r], writes=[ep_r])
                        P.add("dve", lambda e: e.reciprocal(out=ep_r[:rows, 5:6], in_=ep_r[:rows, 4:5]), reads=[ep_r], writes=[ep_r])
                        dt_, dap = dst_fn(hh, qs)
                        P.add("dve", _stt(dap, ey[:rows, :], ep_r[:rows, 5:6], sg8[:rows, :], ALU.mult, ALU.mult),
                              reads=[ey, ep_r, sg8], writes=[dt_])

            class Pipe:
                def __init__(self, depth=1):
                    self.depth = depth
                    self.pending = []
                    self.later = []

                def push(self, s1, s2):
                    s1()
                    self.pending.append(s2)
                    if len(self.pending) > self.depth:
                        self.pending.pop(0)()
                    for it in list(self.later):
                        it[0] -= 1
                        if it[0] <= 0:
                            self.later.remove(it)
                            it[1]()

                def flush(self):
                    while self.pending:
                        self.pending.pop(0)()

                def flush_later(self):
                    for it in self.later:
                        it[1]()
                    self.later = []

            pipe = Pipe(2)
            sbankB = [pf[0], pf[1], pf[5]]
            accs = s2("accs", [128, 8, 130], F32)

            def getaccs(idx, rows, ncol=130):
                return accs, accs[:rows, idx, :]

            NJ = int(os.environ.get("K_J", "16"))
            for j in range(NJ if not os.environ.get("K_NOB") else 0):
                qb_ = qbt[j % 2]
                for c in range(2):
                    P.add("sp", _dma(qb_[c][c * 64:(c + 1) * 64, :, :], qb_scr[:, c * 64:(c + 1) * 64, j * 256:(j + 1) * 256].rearrange("c p t -> p c t")),
                          reads=[t_qb], writes=[qb_[c]], dma=True)
                for hg in range(2):
                    nkp = 4 * j + 4
                    for kbp in range(nkp):
                        rr, jj = kbp % 4, kbp // 4
                        k_ = kt[cnt["k"] % 4]
                        v_ = vt[cnt["k"] % 4]
                        vb2 = vt_b[cnt["k"] % 4]
                        cnt["k"] += 1
                        row0 = (jj // 4) * 4096 + rr * 1024 + (jj % 4) * 256
                        P.add("sp", _dma(k_[:, :, :], kbT_g[row0:row0 + 256, hg * 256:(hg + 1) * 256].rearrange("(b p) x -> p b x", p=128)),
                              reads=[t_kbT_g[jj // 4]], writes=[k_], dma=True)
                        for blk in range(2):
                            P.add("sp", _dma(v_[:, blk, :, 0:128],
                                             vb_g[row0 + blk * 128:row0 + (blk + 1) * 128, hg * 256:(hg + 1) * 256].rearrange("p (hh e) -> p hh e", hh=2)),
                                  reads=[t_vb_g[jj // 4]], writes=[vb2[blk]], dma=True)
                        diag = kbp >= 4 * j
                        for blk in range(2):
                            i_d = 2 * (kbp - 4 * j) + blk
                            oi = 2 * kbp + blk - 8 * j + 120
                            for hh in range(2):
                                h = 2 * hg + hh
                                ps = sbankB[cnt["s"] % 3]
                                cnt["s"] += 1
                                p_ = pt[cnt["pt"] % 4]
                                cnt["pt"] += 1
                                first = (kbp == 0 and blk == 0)

                                def s1(ps=ps, k_=k_, blk=blk, hh=hh, h=h, qb_=qb_, diag=diag, i_d=i_d):
                                    for c in range(2):
                                        P.add("pe", _mmx(ps[:, c * 256:(c + 1) * 256], k_[:, blk, hh * 128:(hh + 1) * 128],
                                                         qb_[c][:, h, :], c == 0), reads=[k_, qb_[c]], writes=[ps])
                                    if diag:
                                        for c in range(2):
                                            P.add("pe", _mmx(ps[:, c * 256:(c + 1) * 256], identb[:, :], dtab[:, h, i_d, :], False),
                                                  reads=[identb, dtab], writes=[ps])

                                def s2_(ps=ps, p_=p_, v_=v_, vtl=vb2[blk], blk=blk, hh=hh, h=h, oi=oi, first=first):
                                    P.add("act", _act(p_[:, :], ps[:, :], AF.Exp, bias=btab[:, h, oi:oi + 1], scale=0.125),
                                          reads=[ps, btab], writes=[p_])
                                    if first and hh == 0:
                                        opened.clear()
                                    for c in range(2):
                                        for qs in range(2):
                                            mm2(hh * 4 + c * 2 + qs, 128, 130, p_[:, c * 256 + qs * 128:c * 256 + (qs + 1) * 128], v_[:, blk, hh, :],
                                                first, [p_, vtl])

                                pipe.push(s1, s2_)
                    pipe.flush()
                    y_ = yb[cnt["yb"] % 2]
                    yT_ = ybT[cnt["yb"] % 2]
                    cnt["yb"] += 1
                    P.add("dve", _copy(accs[:, 0:3, :], pf[2][:, 0:390].rearrange("p (a b) -> p a b", a=3)), reads=[pf[2]], writes=[accs])
                    P.add("act", _act(accs[:, 3:6, :], pf[3][:, 0:390].rearrange("p (a b) -> p a b", a=3), AF.Copy), reads=[pf[3]], writes=[accs])
                    P.add("dve", _copy(accs[:, 6:8, :], pf[4][:, 0:260].rearrange("p (a b) -> p a b", a=2)), reads=[pf[4]], writes=[accs])
                    diff_epilogue(128, [0, 1], 2, lambda hh, qs, y_=y_: (y_, y_[:, qs, hh * 128:(hh + 1) * 128]), getacc=getaccs)

                    def tpose(y_=y_, yT_=yT_, hg=hg, j=j):
                        pbt = pb[hg % 2]
                        for hh in range(2):
                            for qs in range(2):
                                P.add("pe", _tp(pbt[:, hh * 256 + qs * 128:hh * 256 + (qs + 1) * 128], y_[:, qs, hh * 128:(hh + 1) * 128], identb[:, :]),
                                      reads=[y_, identb], writes=[pbt])
                        P.add("act", _act(yT_[:, :, :], pbt[:, 0:512].rearrange("p (a b) -> p a b", a=2), AF.Copy), reads=[pbt], writes=[yT_])
                        P.add("sp", _dma(yT_scr[4 + 2 * hg:6 + 2 * hg, :, j * 256:(j + 1) * 256].rearrange("c p t -> p c t"), yT_[:, :, :]),
                              reads=[yT_], writes=[t_yT], dma=True)

                    pipe.later.append([12, tpose])
            pipe.flush()
            pipe.flush_later()

            pipeA = Pipe(2)
            sbank = [pf[0], pf[1], pf[5]]
            for hp in range(4 if not os.environ.get("K_NOA") else 0):
                P.add("sp", _dma(gt[:, :, :, :].rearrange("p i h q -> p (i h q)"), L["c_gb"][hp, :, :]), writes=[gt], dma=True)
                for hh in range(2):
                    P.add("dve", _tt(gt[:, :, hh, :], gt[:, :, hh, :], maska[:, :, :], ALU.add), reads=[gt, maska], writes=[gt])
                for j in range(NJ):
                    qa_ = qat[j % 2]
                    for hh in range(2):
                        P.add("sp", _dma(qa_[hh][hh * 64:(hh + 1) * 64, :], qa_scr[hp, hh * 64:(hh + 1) * 64, j * 256:(j + 1) * 256]),
                              reads=[t_qa], writes=[qa_[hh]], dma=True)
                    ilist = [i for i in range(12) if 8 * j - 4 + i >= 0]
                    for ip in range(ilist[0] // 2, 6):
                        tt_ = 4 * j - 2 + ip
                        rr, jj = tt_ % 4, tt_ // 4
                        k_ = kat[cnt["k"] % 4]
                        v_ = vat[cnt["k"] % 4]
                        vb2 = vat_b[cnt["k"] % 4]
                        cnt["k"] += 1
                        row0 = (jj // 4) * 4096 + rr * 1024 + (jj % 4) * 256
                        P.add("sp", _dma(k_[:, :, :], kaT_g[row0:row0 + 256, hp * 128:(hp + 1) * 128].rearrange("(b p) x -> p b x", p=128)),
                              reads=[t_kaT_g[jj // 4]], writes=[k_], dma=True)
                        tr0 = row0
                        for blk in range(2):
                            P.add("sp", _dma(v_[:, blk, :, 0:64],
                                             va_g[tr0 + blk * 128:tr0 + (blk + 1) * 128, hp * 128:(hp + 1) * 128].rearrange("p (hh d) -> p hh d", hh=2)),
                                  reads=[t_va_g[jj // 4]], writes=[vb2[blk]], dma=True)
                        for blk in range(2):
                            i = 2 * ip + blk
                            ps = sbank[cnt["s"] % 3]
                            cnt["s"] += 1
                            s_ = sbs[cnt["sb"] % 3]
                            cnt["sb"] += 1
                            p_ = pt[cnt["pt"] % 3]
                            cnt["pt"] += 1
                            first = (ip == ilist[0] // 2 and blk == 0)

                            def s1(ps=ps, k_=k_, blk=blk, qa_=qa_):
                                for hh in range(2):
                                    P.add("pe", _mm(ps[:, hh * 256:(hh + 1) * 256], k_[:, blk, :],
                                                    qa_[hh][:, :], True, True), reads=[k_, qa_[hh]], writes=[ps])

                            def s2_(ps=ps, s_=s_, p_=p_, v_=v_, vtl=vb2[blk], blk=blk, i=i, first=first):
                                P.add("dve", _stt(s_[:, :], ps[:, :], 0.125, gt[:, i, :, :].rearrange("p h q -> p (h q)"), ALU.mult, ALU.add),
                                      reads=[ps, gt], writes=[s_])
                                P.add("act", _act(p_[:, :], s_[:, :], AF.Exp), reads=[s_], writes=[p_])
                                if first:
                                    opened.clear()
                                for hh in range(2):
                                    for qs in range(2):
                                        mm2(hh * 2 + qs, 128, 66, p_[:, hh * 256 + qs * 128:hh * 256 + (qs + 1) * 128], v_[:, blk, hh, :], first, [p_, vtl])

                            pipeA.push(s1, s2_)
                    pipeA.flush()
                    y_ = ya[j % 2]
                    yT_ = yaT[j % 2]
                    P.add("dve", _copy(accs[:, 0:4, 0:66], pf[2][:, 0:264].rearrange("p (a b) -> p a b", a=4)), reads=[pf[2]], writes=[accs])
                    for hh in range(2):
                        for qs in range(2):
                            k = hh * 2 + qs
                            P.add("dve", lambda e, k=k: e.reciprocal(out=ep_r[:, k:k + 1], in_=accs[:, k, 64:65]), reads=[accs], writes=[ep_r])
                            P.add("dve", _ts(y_[:, qs, hh * 64:(hh + 1) * 64], accs[:, k, 0:64], ep_r[:, k:k + 1], None, ALU.mult),
                                  reads=[accs, ep_r], writes=[y_])

                    def tposeA(y_=y_, yT_=yT_, hp=hp, j=j):
                        pbt = pb[j % 2]
                        for qs in range(2):
                            P.add("pe", _tp(pbt[:, qs * 128:(qs + 1) * 128], y_[:, qs, :], identb[:, :]), reads=[y_, identb], writes=[pbt])
                        P.add("act", _act(yT_[:, :], pbt[:, 0:256], AF.Copy), reads=[pbt], writes=[yT_])
                        P.add("sp", _dma(yT_scr[hp, :, j * 256:(j + 1) * 256], yT_[:, :]), reads=[yT_], writes=[t_yT], dma=True)

                    pipeA.later.append([6, tposeA])
            pipeA.flush()
            pipeA.flush_later()

            self.sample_attn(p2, L, locals())
            P.barrier(self.mkbar, {'pe': [self.pf[5]]})

    def sample_attn(self, p2, L, L2):
        P = self.P
        pf, pb, identb = self.pf, self.pb, self.identb
        s2 = lambda name, shape, dt: self.sb(p2, name, shape, dt)
        acc, cnt, diff_epilogue, mm2, opened = L2["acc"], L2["cnt"], L2["diff_epilogue"], L2["mm2"], L2["opened"]
        btabs, dtabs, ep_r, sbs, pt = L2["btabs"], L2["dtabs"], L2["ep_r"], L2["sbs"], L2["pt"]
        cak, cav, cbk, cbv = L["cak"], L["cav"], L["cbk"], L["cbv"]
        qaT_s, kaT_s, qbT_s, kbT_s, va_s, vb_s = L["qaT_s"], L["kaT_s"], L["qbT_s"], L["kbT_s"], L["va_s"], L["vb_s"]
        yT_scr, t_yT = L["yT_scr"], L["t_yT"]
        ckb = [s2("ckb0", [128, 8, 512], BF16)] * 2
        cvb = [s2("cvb0", [128, 8, 4, 130], BF16)] * 2
        kts = [s2("kts%d" % i, [128, 4, 128], BF16) for i in range(3)]
        cka = s2("cka", [128, 4, 512], BF16)
        cva = s2("cva", [128, 4, 8, 66], BF16)
        gs = s2("gs", [128, 8, 4, 16], F32)
        gn = s2("gn", [16, 8, 16], F32)
        ysb = s2("ysb", [16, 4, 128], BF16)
        ysa = s2("ysa", [16, 512], BF16)
        yT_s = s2("yT_s", [128, 8, NS], BF16)
        P.add("sp", _dma(gs[:, :, :, :].rearrange("p h k t -> p (h k t)"), L["c_gs"][:, :]), writes=[gs], dma=True)
        P.add("sp", _dma(gn[:, :, :].rearrange("p h t -> p (h t)"), L["c_gn"][:, :]), writes=[gn], dma=True)
        P.add("dve", _memset(cvb[0][:, :, :, 128:130], 1.0), writes=[cvb[0]])
        cv_b = [Tl(cvb[0].ap, "cvb_k%d" % kk) for kk in range(8)]
        for t_ in cv_b:
            t_.lw = cvb[0].lw
        P.add("dve", _memset(cva[:, :, :, 64:66], 1.0), writes=[cva])
        nb = 4 if not os.environ.get("K_NOS") else 0
        Pipe = L2["Pipe"]
        pipeT = Pipe(1)
        pipeS = Pipe(1)
        sbankS = [pf[0], pf[1], pf[5]]
        for b in range(nb):
            for ch in range(4):
                pipeT.flush()
                pipeS.flush()
                ck = ckb[ch % 2]
                cv = cvb[ch % 2]
                P.add("pool", _dma(ck[:, :, :], cbk[b, ch * 1024:(ch + 1) * 1024, :].rearrange("(k p) c -> p k c", p=128)),
                      writes=[ck], dma=True)
                for kk in range(8):
                    r0 = ch * 1024 + kk * 128
                    P.add("pool", _dma(cv[:, kk, :, 0:128], cbv[b, r0:r0 + 128, :].rearrange("p (h e) -> p h e", h=4)),
                          writes=[cv_b[kk]], dma=True)
                for kk in range(8):
                    blk = ch * 8 + kk
                    pbt = pb[blk % 2]
                    k_ = kts[blk % 3]
                    ps = sbankS[cnt["s"] % 3]
                    cnt["s"] += 1
                    p_ = pt[cnt["pt"] % 4]
                    cnt["pt"] += 1

                    def s0(pbt=pbt, ck=ck, kk=kk, k_=k_):
                        for h in range(4):
                            P.add("pe", _tp(pbt[:, h * 128:(h + 1) * 128], ck[:, kk, h * 128:(h + 1) * 128], identb[:, :]),
                                  reads=[ck, identb], writes=[pbt])
                        P.add("dve", _copy(k_[:, :, :], pbt[:, 0:512].rearrange("p (h k) -> p h k", h=4)), reads=[pbt], writes=[k_])

                    def s1(ps=ps, k_=k_, b=b):
                        for h in range(4):
                            for c in range(2):
                                o = (h * 2 + c) * 16
                                P.add("pe", _mm(ps[:, o:o + 16], k_[:, h, :], qbT_s[c][:, h, b * 16:(b + 1) * 16],
                                                True, True), reads=[k_, qbT_s[c]], writes=[ps])

                    def s2_(ps=ps, p_=p_, blk=blk, cv=cv, kk=kk, cvt=cv_b[kk]):
                        for h in range(4):
                            P.add("act", _act(p_[:, h * 32:(h + 1) * 32], ps[:, h * 32:(h + 1) * 32], AF.Exp, bias=btabs[:, h, blk:blk + 1], scale=0.125),
                                  reads=[ps, btabs], writes=[p_])
                        if blk == 0:
                            opened.clear()
                        for h in range(4):
                            for c in range(2):
                                o = (h * 2 + c) * 16
                                mm2(h * 2 + c, 16, 130, p_[:, o:o + 16], cv[:, kk, h, :], blk == 0, [p_, cvt])

                    pipeT.push(s0, (lambda s1=s1, s2_=s2_: pipeS.push(s1, s2_)))
            pipeT.flush()
            pipeS.flush()
            ps = pf[cnt["s"] % 2]
            cnt["s"] += 1
            for h in range(4):
                for c in range(2):
                    o = (h * 2 + c) * 16
                    P.add("pe", _mmx(ps[:16, o:o + 16], kbT_s[:, h, b * 16:(b + 1) * 16],
                                     qbT_s[c][:, h, b * 16:(b + 1) * 16], h == 0 and c == 0), reads=[kbT_s, qbT_s[c]], writes=[ps])
                    P.add("pe", _mmx(ps[:16, o:o + 16], identb[0:16, 0:16], dtabs[0:16, h, :], False), reads=[identb, dtabs], writes=[ps])
            p_ = pt[cnt["pt"] % 3]
            cnt["pt"] += 1
            P.add("act", _act(p_[:16, 0:128], ps[:16, 0:128], AF.Exp, scale=0.125), reads=[ps], writes=[p_])
            for h in range(4):
                for c in range(2):
                    o = (h * 2 + c) * 16
                    mm2(h * 2 + c, 16, 130, p_[:16, o:o + 16], vb_s[0:16, b, h, :], False, [p_, vb_s])
            diff_epilogue(16, [0, 1, 2, 3], 1, lambda hh, qs: (ysb, ysb[:16, hh, :]),
                          i0=lambda hh, qs: hh * 2, i1=lambda hh, qs: hh * 2 + 1)
            pbt = pb[b % 2]
            for h in range(4):
                P.add("pe", _tp(pbt[:, h * 16:(h + 1) * 16], ysb[:16, h, :], identb[0:16, 0:16]), reads=[ysb, identb], writes=[pbt])
            P.add("act", _act(yT_s[:, 4:8, b * 16:(b + 1) * 16], pbt[:, 0:64].rearrange("p (h t) -> p h t", h=4), AF.Copy),
                  reads=[pbt], writes=[yT_s])
            P.add("pool", _dma(cka[:, :, :], cak[b, 64:576, :].rearrange("(k p) c -> p k c", p=128)), writes=[cka], dma=True)
            for kk in range(4):
                r0 = 64 + kk * 128
                P.add("pool", _dma(cva[:, kk, :, 0:64], cav[b, r0:r0 + 128, :].rearrange("p (h d) -> p h d", h=8)), writes=[cva], dma=True)
            for kk in range(4):
                pbt = pb[kk % 2]
                k_ = kts[kk % 2]
                for hp in range(4):
                    P.add("pe", _tp(pbt[:, hp * 128:(hp + 1) * 128], cka[:, kk, hp * 128:(hp + 1) * 128], identb[:, :]),
                          reads=[cka, identb], writes=[pbt])
                P.add("dve", _copy(k_[:, :, :], pbt[:, 0:512].rearrange("p (h k) -> p h k", h=4)), reads=[pbt], writes=[k_])
                ps = pf[cnt["s"] % 2]
                cnt["s"] += 1
                for h in range(8):
                    P.add("pe", _mm(ps[:, h * 16:(h + 1) * 16], k_[:, h // 2, :],
                                    qaT_s[h % 2][:, h // 2, b * 16:(b + 1) * 16], True, True), reads=[k_, qaT_s[h % 2]], writes=[ps])
                s_ = sbs[cnt["sb"] % 2]
                cnt["sb"] += 1
                P.add("dve", _stt(s_[:, 0:128].rearrange("p (h t) -> p h t", h=8), ps[:, 0:128].rearrange("p (h t) -> p h t", h=8), 0.125,
                                  gs[:, :, kk, :], ALU.mult, ALU.add), reads=[ps, gs], writes=[s_])
                p_ = pt[cnt["pt"] % 3]
                cnt["pt"] += 1
                P.add("act", _act(p_[:, 0:128], s_[:, 0:128], AF.Exp), reads=[s_], writes=[p_])
                if kk == 0:
                    opened.clear()
                for h in range(8):
                    mm2(h, 16, 66, p_[:, h * 16:(h + 1) * 16], cva[:, kk, h, :], kk == 0, [p_, cva])
            ps = pf[cnt["s"] % 2]
            cnt["s"] += 1
            for h in range(8):
                P.add("pe", _mm(ps[:16, h * 16:(h + 1) * 16], kaT_s[:, h // 2, b * 16:(b + 1) * 16],
                                qaT_s[h % 2][:, h // 2, b * 16:(b + 1) * 16], True, True), reads=[kaT_s, qaT_s[h % 2]], writes=[ps])
            s_ = sbs[cnt["sb"] % 2]
            cnt["sb"] += 1
            P.add("dve", _stt(s_[:16, 0:128].rearrange("p (h t) -> p h t", h=8), ps[:16, 0:128].rearrange("p (h t) -> p h t", h=8), 0.125,
                              gn[:, :, :], ALU.mult, ALU.add), reads=[ps, gn], writes=[s_])
            p_ = pt[cnt["pt"] % 3]
            cnt["pt"] += 1
            P.add("act", _act(p_[:16, 0:128], s_[:16, 0:128], AF.Exp), reads=[s_], writes=[p_])
            for h in range(8):
                mm2(h, 16, 66, p_[:16, h * 16:(h + 1) * 16], va_s[0:16, b, h, :], False, [p_, va_s])
            for h in range(8):
                ta, aa = acc(h, 16, 66)
                P.add("dve", lambda e, a=aa: e.reciprocal(out=ep_r[:16, 0:1], in_=a[:, 64:65]), reads=[ta], writes=[ep_r])
                P.add("dve", _ts(ysa[:16, h * 64:(h + 1) * 64], aa[:, 0:64], ep_r[:16, 0:1], None, ALU.mult), reads=[ta, ep_r], writes=[ysa])
            pbt = pb[(b + 1) % 2]
            for hp in range(4):
                P.add("pe", _tp(pbt[:, hp * 16:(hp + 1) * 16], ysa[:16, hp * 128:(hp + 1) * 128], identb[0:16, 0:16]), reads=[ysa, identb], writes=[pbt])
            P.add("act", _act(yT_s[:, 0:4, b * 16:(b + 1) * 16], pbt[:, 0:64].rearrange("p (h t) -> p h t", h=4), AF.Copy),
                  reads=[pbt], writes=[yT_s])
        if nb:
            P.add("sp", _dma(yT_scr[:, :, NPT:NPT + NS].rearrange("c p t -> p c t"), yT_s[:, :, :]), reads=[yT_s], writes=[t_yT], dma=True)

    def phase3(self, top, L):
        P = self.P
        pf, pb = self.pf, self.pb
        h_scr, g_scr, yT_scr = L["h_scr"], L["g_scr"], L["yT_scr"]
        t_h, t_g, t_yT = L["t_h"], L["t_g"], L["t_yT"]
        p3 = ExitStack()
        with p3:
            s3 = lambda name, shape, dt: self.sb(p3, name, shape, dt)
            h_t = s3("h_t3", [128, 4, D], F32)
            y_t = s3("y_t3", [128, 4, D], F32)
            xT = s3("xT3", [128, 8, 512], BF16)
            self.aT = s3("aT3", [128, 22, 512], BF16)
            yTs = [s3("yT3_0", [128, 8, 512], BF16)] * 2
            gts = [s3("gts%d" % i, [128, 2048], BF16) for i in range(2)]
            qmT = s3("qmT", [128, 8, 512], BF16)
            omT = s3("omT", [128, 8, 512], BF16)
            ptm = [s3("ptm%d" % i, [128, 512], BF16) for i in range(2)]
            t1 = s3("t1", [128, 512], F32)
            rl = t1
            t2 = s3("t2", [128, 512], F32)
            g3_ = s3("g3_", [128, D], F32)
            g4_ = s3("g4_", [128, D], F32)
            g5_ = s3("g5_", [128, D], F32)
            g7_ = s3("g7_", [128, D], F32)
            g8_ = s3("g8_", [128, D], F32)
            self.load_gain(g3_, 3, 32.0)
            self.load_gain(g4_, 4, 32.0)
            self.load_gain(g5_, 5, 32.0)
            self.load_gain(g7_, 7, 32.0)
            self.load_gain(g8_, 8, 16.0)
            mkT = s3("mkT", [128, 8, 256], BF16)
            mvb = s3("mvb", [128, 2, D], BF16)
            ckm = s3("ckm", [128, 2, D], BF16)
            ones_b = s3("ones_b", [128, 128], BF16)
            P.add("dve", _memset(ones_b[:, :], 1.0), writes=[ones_b])
            P.add("sp", _dma(mkT[:, :, :], L["mkT_scr"].rearrange("c p m -> p c m")), reads=[L["t_mkT"]], writes=[mkT], dma=True)
            P.add("sp", _dma(mvb[:, :, :], L["mv_scr"].rearrange("(mb p) d -> p mb d", p=128)), reads=[L["t_mv"]], writes=[mvb], dma=True)

            def mem_attend(c0, n):
                for hm in range(4):
                    for mb in range(2):
                        ps = pf[mb]
                        for dc in range(2):
                            P.add("pe", _mm(ps[:, 0:n], mkT[:, hm * 2 + dc, mb * 128:(mb + 1) * 128], qmT[:, hm * 2 + dc, c0:c0 + n], dc == 0, dc == 1),
                                  reads=[mkT, qmT], writes=[ps])
                        P.add("act", _act(ptm[mb][:, 0:n], ps[:, 0:n], AF.Exp, scale=1.0 / 16.0), reads=[ps], writes=[ptm[mb]])
                    for dc in range(2):
                        ps = pf[2 + dc]
                        for mb in range(2):
                            P.add("pe", _mm(ps[:, 0:n], mvb[:, mb, hm * 256 + dc * 128:hm * 256 + (dc + 1) * 128], ptm[mb][:, 0:n], mb == 0, mb == 1),
                                  reads=[mvb, ptm[mb]], writes=[ps])
                    ps = pf[4]
                    for mb in range(2):
                        P.add("pe", _mm(ps[:, 0:n], ones_b[:, :], ptm[mb][:, 0:n], mb == 0, mb == 1), reads=[ones_b, ptm[mb]], writes=[ps])
                    P.add("dve", lambda e, ps=ps: e.reciprocal(out=rl[:, 0:n], in_=ps[:, 0:n]), reads=[ps], writes=[rl])
                    for dc in range(2):
                        P.add("dve", _tt(omT[:, hm * 2 + dc, c0:c0 + n], pf[2 + dc][:, 0:n], rl[:, 0:n], ALU.mult),
                              reads=[pf[2 + dc], rl], writes=[omT])

            tiles = [(i, 512) for i in range(8)] + [(8, NS)]
            if os.environ.get("K_TILES3"):
                tiles = [tiles[int(i)] for i in os.environ["K_TILES3"].split(",") if i != "x"]
            def load_yT(idx3):
                ti_, ntok_ = tiles[idx3]
                P.add("sp", _dma(yTs[idx3 % 2][:, :, 0:ntok_], yT_scr[:, :, ti_ * 512:ti_ * 512 + ntok_].rearrange("c p t -> p c t")),
                      reads=[t_yT], writes=[yTs[idx3 % 2]], dma=True)

            for idx3, (ti, ntok) in enumerate(tiles):
                samp = ti == 8
                rows = min(128, ntok)
                TB = max(1, ntok // 128)
                tok0 = ti * 512
                yT = yTs[idx3 % 2]
                load_yT(idx3)
                P.add("sp", _dma(h_t[:rows, 0:TB, :], h_scr[tok0:tok0 + ntok, :].rearrange("(tb p) d -> p tb d", p=rows)),
                      reads=[t_h], writes=[h_t], dma=True)
                wb = self.wbuf[self.wcnt % 2]
                self.wcnt += 1
                wv = wb[:, 0:8192].rearrange("p (k c) -> p k c", k=8)
                self.load_w(wb, wv[:, 0:4, :], L["w_br_a"], 4)
                self.load_w(wb, wv[:, 4:8, :], L["w_br_b"], 4)
                for tb in range(TB):
                    g_ = gts[tb % 2]
                    r0 = tok0 + tb * rows
                    P.add("sp", _dma(g_[:rows, :], g_scr[r0:r0 + rows, :]), reads=[t_g], writes=[g_], dma=True)
                    mg = self.xn[tb % 2]
                    for half in range(2):
                        psA = pf[2 * half]
                        psB = pf[2 * half + 1]
                        for kc in range(4):
                            P.add("pe", _mm(psA[:rows, :], yT[:, kc, tb * rows:(tb + 1) * rows], wv[:, kc, half * 512:(half + 1) * 512], kc == 0, kc == 3),
                                  reads=[yT, wb], writes=[psA])
                        for kc in range(4):
                            P.add("pe", _mm(psB[:rows, :], yT[:, 4 + kc, tb * rows:(tb + 1) * rows], wv[:, 4 + kc, half * 512:(half + 1) * 512], kc == 0, kc == 3),
                                  reads=[yT, wb], writes=[psB])
                        P.add("dve", _tt(t1[:rows, :], psA[:rows, :], g_[:rows, half * 512:(half + 1) * 512], ALU.mult), reads=[psA, g_], writes=[t1])
                        P.add("dve", _tt(t2[:rows, :], psB[:rows, :], g_[:rows, 1024 + half * 512:1024 + (half + 1) * 512], ALU.mult),
                              reads=[psB, g_], writes=[t2])
                        P.add("dve", _tt(mg[:rows, half * 512:(half + 1) * 512], t1[:rows, :], t2[:rows, :], ALU.add), reads=[t1, t2], writes=[mg])
                    pbt = pb[tb % 2]
                    for kc in range(8):
                        P.add("pe", _tp(pbt[:, kc * 128:kc * 128 + rows], mg[:rows, kc * 128:(kc + 1) * 128], self.identb[:rows, :rows]),
                              reads=[mg, self.identb], writes=[pbt])
                    P.add("act", _act(xT[:, :, tb * rows:(tb + 1) * rows], pbt[:, 0:1024].rearrange("p (k c) -> p k c", k=8)[:, :, 0:rows], AF.Copy),
                          reads=[pbt], writes=[xT])
                self.linear_tm(xT, rows, TB, L["w_out"], y_t)
                self.resid_norm(h_t, y_t, rows, TB, g3_)
                self.rms_T(h_t, rows, TB, g4_, xT)
                wb = self.wbuf[self.wcnt % 2]
                self.wcnt += 1
                wv = wb[:, 0:8192].rearrange("p (k c) -> p k c", k=8)
                self.load_w(wb, wv, L["w_mq"], 8)
                for c in range(8):
                    ps = pf[c % 4]
                    for kc in range(8):
                        P.add("pe", _mm(ps[:, 0:ntok], wv[:, kc, c * 128:(c + 1) * 128], xT[:, kc, 0:ntok], kc == 0, kc == 7), reads=[wb, xT], writes=[ps])
                    if c % 2 == 0:
                        P.add("act", _act(qmT[:, c, 0:ntok], ps[:, 0:ntok], AF.Copy), reads=[ps], writes=[qmT])
                    else:
                        P.add("dve", _copy(qmT[:, c, 0:ntok], ps[:, 0:ntok]), reads=[ps], writes=[qmT])
                if not samp:
                    mem_attend(0, ntok)
                else:
                    for b in range(4):
                        P.add("pool", _dma(ckm[:, :, :], L["cmk"][b, :, :].rearrange("(mb p) d -> p mb d", p=128)), writes=[ckm], dma=True)
                        P.add("pool", _dma(mvb[:, :, :], L["cmv"][b, :, :].rearrange("(mb p) d -> p mb d", p=128)), writes=[mvb], dma=True)
                        for mb in range(2):
                            pbt = pb[mb]
                            for c in range(8):
                                P.add("pe", _tp(pbt[:, c * 128:(c + 1) * 128], ckm[:, mb, c * 128:(c + 1) * 128], self.identb[:, :]),
                                      reads=[ckm, self.identb], writes=[pbt])
                            P.add("dve", _copy(mkT[:, :, mb * 128:(mb + 1) * 128], pbt[:, 0:1024].rearrange("p (c m) -> p c m", c=8)),
                                  reads=[pbt], writes=[mkT])
                        mem_attend(b * 16, 16)
                self.linear_tm(omT, rows, TB, L["w_mo"], y_t)
                self.resid_norm(h_t, y_t, rows, TB, g5_)
                self.rms_T(h_t, rows, TB, g7_, xT)
                self.swiglu_ffn(xT, ntok, rows, TB, L["f2u"], L["f2d"], y_t)
                self.resid_norm(h_t, y_t, rows, TB, g8_)
                outt = L["y_s"] if samp else L["y_p"][tok0:tok0 + ntok, :]
                P.add("sp", _dma(outt.rearrange("(tb p) d -> p tb d", p=rows), h_t[:rows, 0:TB, :]), reads=[h_t], dma=True, out=True)

    def linear_tm(self, xT, rows, TB, w, y_t):
        P = self.P
        wb = self.wbuf[self.wcnt % 2]
        self.wcnt += 1
        wv = wb[:, 0:8192].rearrange("p (k c) -> p k c", k=8)
        self.load_w(wb, wv, w, 8)
        for tb in range(TB):
            for half in range(2):
                ps = self.pf[4 + half]
                for kc in range(8):
                    P.add("pe", _mm(ps[:rows, :], xT[:, kc, tb * rows:(tb + 1) * rows], wv[:, kc, half * 512:(half + 1) * 512], kc == 0, kc == 7),
                          reads=[xT, wb], writes=[ps])
                P.add("act", _act(y_t[:rows, tb, half * 512:(half + 1) * 512], ps[:rows, :], AF.Copy), reads=[ps], writes=[y_t])


    def build(self):
        nc = self.nc
        P = self.P
        xp = self.din("xp", [NPT, D])
        xs = self.din("xs", [NS, D])
        cak = self.din("cak", [4, 576, 512])
        cav = self.din("cav", [4, 576, 512])
        cbk = self.din("cbk", [4, 4096, 512])
        cbv = self.din("cbv", [4, 4096, 512])
        cmk = self.din("cmk", [4, 256, 1024])
        cmv = self.din("cmv", [4, 256, 1024])
        memp = self.din("memp", [256, D])
        w_in = self.din("w_in", [D, 3072])
        w_gate = self.din("w_gate", [D, 2048])
        b_gate = self.din("b_gate", [1, 2048])
        rel_bias = self.din("rel_bias", [8, 257])
        lam_qk = self.din("lam_qk", [1, 256])
        subln_g = self.din("subln_g", [1, 128])
        w_br_a = self.din("w_br_a", [512, D])
        w_br_b = self.din("w_br_b", [512, D])
        w_out = self.din("w_out", [D, D])
        w_mq = self.din("w_mq", [D, D])
        w_mkv = self.din("w_mkv", [D, 2048])
        w_mo = self.din("w_mo", [D, D])
        self.norm_g = self.din("norm_g", [9, D])
        f1u = self.din("f1u", [D, 2 * FF])
        f1d = self.din("f1d", [FF, D])
        f2u = self.din("f2u", [D, 2 * FF])
        f2d = self.din("f2d", [FF, D])
        self.wf32 = {}
        self.wtl = {}
        Lw = locals()
        twins = {}
        for nm in ["w_in", "w_gate", "w_br_a", "w_br_b", "w_out", "w_mq", "w_mkv", "w_mo", "f1u", "f1d", "f2u", "f2d"]:
            f32ap = Lw[nm]
            bfap = self.dint(nm + "_bf", list(f32ap.shape), BF16)
            self.wf32[nm] = (f32ap, bfap)
            self.wtl[bfap.tensor.name] = []
            twins[nm] = bfap
        w_in, w_gate, w_br_a, w_br_b = twins["w_in"], twins["w_gate"], twins["w_br_a"], twins["w_br_b"]
        w_out, w_mq, w_mkv, w_mo = twins["w_out"], twins["w_mq"], twins["w_mkv"], twins["w_mo"]
        f1u, f1d, f2u, f2d = twins["f1u"], twins["f1d"], twins["f2u"], twins["f2d"]
        c_identb = self.din("c_identb", [128, 128], BF16)
        c_identf = self.din("c_identf", [128, 128])
        c_jf = self.din("c_jf", [128, 128])
        c_btab = self.din("c_btab", [128, 4 * 128])
        c_dtab = self.din("c_dtab", [128, 4 * 8 * 256], BF16)
        c_btabs = self.din("c_btabs", [128, 4 * 33])
        c_dtabs = self.din("c_dtabs", [16, 4 * 16], BF16)
        c_maska = self.din("c_maska", [128, 12 * 256])
        c_gb = self.din("c_gb", [4, 128, 12 * 2 * 256])
        c_gs = self.din("c_gs", [128, 8 * 4 * 16])
        c_gn = self.din("c_gn", [16, 8 * 16])

        y_p = self.dout("y_p", [NPT, D])
        y_s = self.dout("y_s", [NS, D])
        ak_last = self.dout("ak_last", [256, 512])
        av_last = self.dout("av_last", [256, 512])
        bk_p = self.dout("bk_p", [NPT, 512])
        bv_p = self.dout("bv_p", [NPT, 512])
        mk_p = self.dout("mk_p", [256, D])
        mv_p = self.dout("mv_p", [256, D])
        ak_s = self.dout("ak_s", [4, 576, 512])
        av_s = self.dout("av_s", [4, 576, 512])
        bk_s = self.dout("bk_s", [NS, 512])
        bv_s = self.dout("bv_s", [NS, 512])

        h_scr = self.dint("h_scr", [NTOK, D], F32)
        g_scr = self.dint("g_scr", [NTOK, 2048], BF16)
        qa_scr = self.dint("qa_scr", [4, 128, NPT], BF16)
        qb_scr = self.dint("qb_scr", [4, 128, NPT], BF16)
        yT_scr = self.dint("yT_scr", [8, 128, NTOK], BF16)
        kaT_in = self.dint("kaT_in", [32 * 128, 512], BF16)
        kbT_in = self.dint("kbT_in", [32 * 128, 512], BF16)
        va_in = self.dint("va_in", [NPT, 512], BF16)
        vb_in = self.dint("vb_in", [NPT, 512], BF16)
        kaT_g = self.dint("kaT_g", [4 * 32 * 128, 512], BF16)
        kbT_g = self.dint("kbT_g", [4 * 32 * 128, 512], BF16)
        va_g = self.dint("va_g", [4 * NPT, 512], BF16)
        vb_g = self.dint("vb_g", [4 * NPT, 512], BF16)
        mkT_scr = self.dint("mkT_scr", [8, 128, 256], BF16)
        mv_scr = self.dint("mv_scr", [256, D], BF16)
        t_mkT = Tl(None, "mkT_scr")
        t_mv = Tl(None, "mv_scr")
        t_h = Tl(None, "h_scr")
        t_g = Tl(None, "g_scr")
        t_qa = Tl(None, "qa_scr")
        t_qb = Tl(None, "qb_scr")
        t_yT = Tl(None, "yT_scr")
        t_kaT_in = [[] for i in range(4)]
        t_kbT_in = [[] for i in range(4)]
        t_va_in = [[] for i in range(4)]
        t_vb_in = [[] for i in range(4)]
        t_kaT_g = [Tl(None, "kaT_g%d" % i) for i in range(4)]
        t_kbT_g = [Tl(None, "kbT_g%d" % i) for i in range(4)]
        t_va_g = [Tl(None, "va_g%d" % i) for i in range(4)]
        t_vb_g = [Tl(None, "vb_g%d" % i) for i in range(4)]

        top = ExitStack()
        with top:
            sb = lambda name, shape, dt: self.sb(top, name, shape, dt)
            self.identb = sb("identb", [128, 128], BF16)
            self.identf = sb("identf", [128, 128], F32)
            self.jf = sb("jf", [128, 128], F32)
            self.ss = sb("ss", [128, 8], F32)
            self.rs = sb("rs", [128, 8], F32)
            self.junk = sb("junk", [128, 1024], BF16)
            self.xn = [sb("xn%d" % i, [128, 1024], BF16) for i in range(2)]
            self.sg = [sb("sg%d" % i, [128, 512], F32) for i in range(2)]
            self.wbuf = [sb("wbuf%d" % i, [128, 12288], BF16) for i in range(2)]
            self.wcnt = 0
            ones2 = sb("ones2", [2, 128], BF16)
            bar_t = {e: sb("bar_" + e, [128, 8], F32) for e in ENGS}
            bar_t2 = {e: sb("bar2_" + e, [128, 8], F32) for e in ENGS}
            bar_src = sb("bar_src", [128, 8], F32)
            qaT_s = [sb("qaT_s%d" % i, [128, 4, NS], BF16) for i in range(2)]
            kaT_s = sb("kaT_s", [128, 4, NS], BF16)
            qbT_s = [sb("qbT_s%d" % i, [128, 4, NS], BF16) for i in range(2)]
            kbT_s = sb("kbT_s", [128, 4, NS], BF16)
            va_s = sb("va_s", [16, 4, 8, 66], BF16)
            vb_s = sb("vb_s", [16, 4, 4, 130], BF16)
            self.pf = [Tl(top.enter_context(nc.psum_tensor("pf%d" % i, [128, 512], F32)), "pf%d" % i) for i in range(6)]
            self.pb = [Tl(top.enter_context(nc.psum_tensor("pb%d" % i, [128, 1024], BF16)), "pb%d" % i) for i in range(2)]

            P.add("sp", _dma(self.identb[:, :], c_identb[:, :]), writes=[self.identb], dma=True)
            P.add("sp", _dma(self.identf[:, :], c_identf[:, :]), writes=[self.identf], dma=True)
            P.add("sp", _dma(self.jf[:, :], c_jf[:, :]), writes=[self.jf], dma=True)
            P.add("dve", _memset(ones2[:, :], 1.0), writes=[ones2])
            P.add("dve", _memset(bar_src[:, :], 0.0), writes=[bar_src])
            for t_ in qaT_s + qbT_s:
                P.add("dve", _memset(t_[:, :, :], 0.0), writes=[t_])
            P.add("dve", _memset(va_s[:, :, :, 64:66], 1.0), writes=[va_s])
            P.add("dve", _memset(vb_s[:, :, :, 128:130], 1.0), writes=[vb_s])

            def mkbar(e, ph):
                t = bar_t[e] if ph == 0 else bar_t2[e]
                if e == "pe":
                    return t, _mm(self.pf[5][0:8, 0:8], self.identb[0:8, 0:8], self.identb[0:8, 0:8], True, True)
                if e == "sp":
                    return t, _dma(t[:, :], bar_src[:, :])
                if e == "act":
                    return t, _act(t[:, :], bar_src[:, :], AF.Copy)
                return t, _copy(t[:, :], bar_src[:, :])

            self.mkbar = mkbar
            for nm in ["w_mkv", "f1u", "f1d", "w_in", "w_gate"]:
                self.convert_w(nm)
            p1 = ExitStack()
            with p1:
                s1 = lambda name, shape, dt: self.sb(p1, name, shape, dt)
                x_t = s1("x_t", [128, 4, D], F32)
                x_t2 = s1("x_t2", [128, 4, D], F32)
                x_ts = [x_t, x_t2]
                y_t = s1("y_t", [128, 4, D], F32)
                xT = s1("xT", [128, 8, 512], BF16)
                self.aT = s1("aT", [128, 22, 512], BF16)
                g0 = s1("g0", [128, D], F32)
                g1 = s1("g1", [128, D], F32)
                g2 = s1("g2", [128, D], F32)
                qst = [s1("qst%d" % i, [128, 4, 512], BF16) for i in range(2)]
                kvst = [s1("kvst0", [128, 4, 512], F32)] * 2
                vst = [s1("vst0", [128, 4, 512], BF16)] * 2
                gst = [s1("gst%d" % i, [128, 1536], BF16) for i in range(2)]
                bg2 = s1("bg2", [2, 2048], BF16)
                self.load_gain(g0, 0, 32.0)
                self.load_gain(g2, 2, 32.0)
                self.load_gain(g1, 6, 32.0)
                bg_f = y_t[0:2, 0:2, :].rearrange("p a b -> p (a b)")
                bg_h = vst[0][0:2, :, :].rearrange("p a b -> p (a b)")
                P.add("sp", _dma(bg_f[0:1, :], b_gate[0:1, :]), writes=[y_t], dma=True)
                P.add("sp", _dma(bg_f[1:2, :], b_gate[0:1, :]), writes=[y_t], dma=True)
                P.add("dve", _copy(bg_h, bg_f), reads=[y_t], writes=[vst[0]])
                P.add("dve", _copy(bg2[:, :], bg_h), reads=[vst[0]], writes=[bg2])
                P.add("dve", _tt(bg_f, bg_f, bg_h, ALU.subtract), reads=[y_t, vst[0]], writes=[y_t])
                P.add("dve", _copy(bg_h, bg_f), reads=[y_t], writes=[vst[0]])
                P.add("sp", _dma(bg2[1:2, :], bg_h[1:2, :]), reads=[vst[0]], writes=[bg2], dma=True)

                if not os.environ.get("K_NOMEM"):
                    self.mem_kv(memp, g1, w_mkv, mk_p, mv_p, x_t, xT, kvst[0], qst[0], mkT_scr, t_mkT, mv_scr, t_mv)
                self.load_gain(g1, 1, 16.0)

                for b in range(4 if not os.environ.get("K_NOSHIFT") else 0):
                    P.add("sp", _dma(ak_s[b, 0:560, :], cak[b, 16:576, :]), dma=True, out=True)
                    P.add("sp", _dma(av_s[b, 0:560, :], cav[b, 16:576, :]), dma=True, out=True)
                tiles = [(i, 512, xp[i * 512:(i + 1) * 512, :]) for i in range(8)] + [(8, NS, xs[:, :])]
                if os.environ.get("K_TILES"):
                    tiles = [tiles[int(i)] for i in os.environ["K_TILES"].split(",") if i != "x"]
                kcnt = 0
                def load_x(idx1):
                    ti_, ntok_, src_ = tiles[idx1]
                    rows_ = min(128, ntok_)
                    TB_ = max(1, ntok_ // 128)
                    xt_ = x_ts[idx1 % 2]
                    P.add("sp", _dma(xt_[:rows_, 0:TB_, :], src_.rearrange("(tb p) d -> p tb d", p=rows_)), writes=[xt_], dma=True)

                if tiles:
                    load_x(0)
                for idx1, (ti, ntok, src) in enumerate(tiles):
                    samp = ti == 8
                    rows = min(128, ntok)
                    TB = max(1, ntok // 128)
                    tok0 = ti * 512
                    x_t = x_ts[idx1 % 2]
                    if idx1 + 1 < len(tiles):
                        load_x(idx1 + 1)
                    PARTS = os.environ.get("K_PARTS", "ffn,res,win,gates").split(",")
                    self.rms_T(x_t, rows, TB, g0, xT)
                    if "ffn" in PARTS:
                        self.swiglu_ffn(xT, ntok, rows, TB, f1u, f1d, y_t)
                    if "res" in PARTS:
                        self.resid_norm(x_t, y_t, rows, TB, g1)
                    P.add("sp", _dma(h_scr[tok0:tok0 + ntok, :].rearrange("(tb p) d -> p tb d", p=rows),
                                     x_t[:rows, 0:TB, :]), reads=[x_t], writes=[Tl(None, 'w_h')], dma=True)
                    self.rms_T(x_t, rows, TB, g2, xT)
                    for grp in range(2 if "win" in PARTS else 0):
                        wb = self.wbuf[self.wcnt % 2]
                        self.wcnt += 1
                        wv = wb[:, 0:12288].rearrange("p (k c) -> p k c", k=8)
                        self.load_w(wb, wv, w_in[:, grp * 1536:(grp + 1) * 1536], 8)
                        WIN = os.environ.get("K_WIN", "fm,tm,fmd,tmd").split(",")
                        for which in range(2 if "fm" in WIN else 0):
                            st_ = qst[kcnt % 2]
                            kcnt += 1
                            for c in range(4):
                                ps = self.pf[c % 4]
                                cc0 = (which * 4 + c) * 128
                                for kc in range(8):
                                    P.add("pe", _mm(ps[:, 0:ntok], wv[:, kc, cc0:cc0 + 128], xT[:, kc, 0:ntok], kc == 0, kc == 7),
                                          reads=[wb, xT], writes=[ps])
                                if samp and which == 0:
                                    dq = qaT_s if grp == 0 else qbT_s
                                    P.add("act", _act(dq[0][0:64, c, :], ps[0:64, 0:ntok], AF.Copy), reads=[ps], writes=[dq[0]])
                                    P.add("act", _act(dq[1][64:128, c, :], ps[64:128, 0:ntok], AF.Copy), reads=[ps], writes=[dq[1]])
                                elif samp:
                                    dst = kaT_s if grp == 0 else kbT_s
                                    P.add("act", _act(dst[:, c, :], ps[:, 0:ntok], AF.Copy), reads=[ps], writes=[dst])
                                elif c % 2 == 0:
                                    P.add("act", _act(st_[:, c, :], ps[:, :], AF.Copy), reads=[ps], writes=[st_])
                                else:
                                    P.add("dve", _copy(st_[:, c, :], ps[:, :]), reads=[ps], writes=[st_])
                            if samp or "fmd" not in WIN:
                                continue
                            if which == 0:
                                dscr, dtl = (qa_scr, t_qa) if grp == 0 else (qb_scr, t_qb)
                                P.add("sp", _dma(dscr[:, :, tok0:tok0 + 512].rearrange("c p t -> p c t"), st_[:, :, :]),
                                      reads=[st_], writes=[Tl(None, 'w_q')], dma=True)
                            else:
                                dscr, dtl = (kaT_in, t_kaT_in[ti // 2]) if grp == 0 else (kbT_in, t_kbT_in[ti // 2])
                                for b in range(4):
                                    b0 = ti * 4 + b
                                    dst = dscr[b0 * 128:(b0 + 1) * 128, :].rearrange("p (c k) -> p c k", c=4)
                                    wt = Tl(None, "w_kT")
                                    dtl.append(wt)
                                    P.add("sp", _dma(dst, st_[:, :, b * 128:(b + 1) * 128]),
                                          reads=[st_], writes=[wt], dma=True)
                        for which in range(2 if "tm" in WIN else 0):
                            kv = kvst[kcnt % 2]
                            vb_ = vst[kcnt % 2]
                            kcnt += 1
                            cc0 = 512 + which * 512
                            for tb in range(TB):
                                ps = self.pf[4 + tb % 2]
                                for kc in range(8):
                                    P.add("pe", _mm(ps[:rows, :], xT[:, kc, tb * rows:(tb + 1) * rows], wv[:, kc, cc0:cc0 + 512],
                                                    kc == 0, kc == 7), reads=[wb, xT], writes=[ps])
                                P.add("dve", _copy(kv[:rows, tb, :], ps[:rows, :]), reads=[ps], writes=[kv])
                                if which == 1 and not samp:
                                    P.add("act", _act(vb_[:rows, tb, :], kv[:rows, tb, :], AF.Copy), reads=[kv], writes=[vb_])
                            if samp:
                                outt = [[ak_s, av_s], [bk_s, bv_s]][grp][which]
                                if grp == 0:
                                    for b in range(4):
                                        P.add("sp", _dma(outt[b, 560:576, :], kv[b * 16:(b + 1) * 16, 0, :]), reads=[kv], dma=True, out=True)
                                else:
                                    P.add("sp", _dma(outt[:, :], kv[:NS, 0, :]), reads=[kv], dma=True, out=True)
                                if which == 1:
                                    for b in range(4):
                                        ps = self.pf[4 + b % 2]
                                        for kc in range(8):
                                            P.add("pe", _mm(ps[:16, :], xT[:, kc, b * 16:(b + 1) * 16], wv[:, kc, cc0:cc0 + 512],
                                                            kc == 0, kc == 7), reads=[wb, xT], writes=[ps])
                                        if grp == 0:
                                            P.add("dve", _copy(va_s[:, b, :, 0:64], ps[:16, :].rearrange("p (h d) -> p h d", h=8)),
                                                  reads=[ps], writes=[va_s])
                                        else:
                                            P.add("dve", _copy(vb_s[:, b, :, 0:128], ps[:16, :].rearrange("p (h d) -> p h d", h=4)),
                                                  reads=[ps], writes=[vb_s])
                                continue
                            if "tmd" not in WIN:
                                continue
                            if grp == 1:
                                outt = bk_p if which == 0 else bv_p
                                P.add("sp", _dma(outt[tok0:tok0 + 512, :].rearrange("(tb p) d -> p tb d", p=128), kv[:, :, :]),
                                      reads=[kv], dma=True, out=True)
                            elif ti == 7:
                                outt = ak_last if which == 0 else av_last
                                P.add("sp", _dma(outt[:, :].rearrange("(tb p) d -> p tb d", p=128), kv[:, 2:4, :]),
                                      reads=[kv], dma=True, out=True)
                            if which == 1:
                                dscr, dtl = (va_in, t_va_in[ti // 2]) if grp == 0 else (vb_in, t_vb_in[ti // 2])
                                wt = Tl(None, "w_v")
                                dtl.append(wt)
                                P.add("sp", _dma(dscr[tok0:tok0 + 512, :].rearrange("(tb p) d -> p tb d", p=128), vb_[:, :, :]),
                                      reads=[vb_], writes=[wt], dma=True)
                    gcnt = 0
                    for grp in range(2 if "gates" in PARTS else 0):
                        wb = self.wbuf[self.wcnt % 2]
                        self.wcnt += 1
                        ncol = 1536 if grp == 0 else 512
                        wv = wb[:, 0:12288].rearrange("p (k c) -> p k c", k=8)
                        self.load_w(wb, wv[:, :, 0:ncol], w_gate[:, grp * 1536:grp * 1536 + ncol], 8)
                        for tb in range(TB):
                            gs = gst[gcnt % 2]
                            gcnt += 1
                            for pc in range(ncol // 512):
                                ps = self.pf[(tb * 3 + pc) % 4]
                                gc0 = grp * 1536 + pc * 512
                                for kc in range(8):
                                    P.add("pe", _mm(ps[:rows, :], xT[:, kc, tb * rows:(tb + 1) * rows], wv[:, kc, pc * 512:(pc + 1) * 512],
                                                    kc == 0, False), reads=[wb, xT], writes=[ps])
                                P.add("pe", _mm(ps[:rows, :], ones2[:, 0:rows], bg2[:, gc0:gc0 + 512], False, True),
                                      reads=[ones2, bg2], writes=[ps])
                                P.add("act", _act(gs[:rows, pc * 512:(pc + 1) * 512], ps[:rows, :], AF.Sigmoid), reads=[ps], writes=[gs])
                            r0 = tok0 + tb * rows
                            P.add("sp", _dma(g_scr[r0:r0 + rows, grp * 1536:grp * 1536 + ncol], gs[:rows, 0:ncol]),
                                  reads=[gs], writes=[Tl(None, 'w_g')], dma=True)
                    if (not samp) and ti % 2 == 1:
                        self.exchange(ti // 2, locals())
                    if ti == tiles[0][0]:
                        for nm in ["w_br_a", "w_br_b", "w_out", "w_mq", "w_mo", "f2u", "f2d"]:
                            self.convert_w(nm)
                P.barrier(mkbar, {'pe': [self.pf[5]]})
            self.phase23(top, locals())
            P.emit(nc, top)
        return nc


_NC = None


def _get_nc():
    global _NC
    if _NC is None:
        _NC = K().build()
    return _NC


def _consts(r):
    bf = ml_dtypes.bfloat16
    c = {}
    c["c_identb"] = np.eye(128, dtype=np.float32).astype(bf)
    c["c_identf"] = np.eye(128, dtype=np.float32)
    c["c_jf"] = np.ascontiguousarray(np.eye(128, dtype=np.float32)[:, ::-1])
    p = np.arange(128, dtype=np.float64)[:, None, None]
    sl = np.array(SLOPES, dtype=np.float64)[None, :, None]
    oi = np.arange(128, dtype=np.float64)[None, None, :]
    c["c_btab"] = (sl * (128.0 * (oi - 120.0) + p - 256.0 * (r + 1))).astype(np.float32).reshape(128, 512)
    i = np.arange(8)[None, None, :, None]
    q = np.arange(256)[None, None, None, :]
    pk = 128 * i + np.arange(128)[:, None, None, None]
    pq = 256 * r + q
    sl4 = np.array(SLOPES, dtype=np.float64)[None, :, None, None]
    d8 = -16.0 * sl4 * np.maximum(pk - pq, 0)
    d8 = np.where((pk // 64) > (pq // 64), NEG, d8)
    c["c_dtab"] = d8.astype(np.float32).astype(bf).reshape(128, 4 * 8 * 256)
    kb = np.arange(33, dtype=np.float64)[None, None, :]
    bs = sl * (128.0 * kb + p - 4096.0)
    bs[:, :, 32] = 0.0
    c["c_btabs"] = bs.astype(np.float32).reshape(128, 4 * 33)
    tk = np.arange(16)[:, None, None]
    tq = np.arange(16)[None, None, :]
    ds_ = 8.0 * np.array(SLOPES)[None, :, None] * (tq - np.abs(tq - tk))
    c["c_dtabs"] = ds_.astype(np.float32).astype(bf).reshape(16, 64)
    ia = np.arange(12)[None, :, None]
    pka = 128 * (ia - 4) + np.arange(128)[:, None, None]
    pqa = 256 * r + np.arange(256)[None, None, :]
    ck = np.floor_divide(pka, 64)
    cq = pqa // 64
    ok = (ck <= cq) & (ck >= cq - 8)
    c["c_maska"] = np.where(ok, 0.0, NEG).astype(np.float32).reshape(128, 12 * 256)
    return c


def _relbias_tables(rb, r):
    p = np.arange(128)
    i = np.arange(12)[None, :, None]
    q = np.arange(256)[None, None, :]
    delta = q - p[:, None, None] + 128 * (2 * r + 4 - i)
    idx = np.clip(delta, -128, 128) + 128
    g = rb[:, idx]
    gb = g.reshape(4, 2, 128, 12, 256).transpose(0, 2, 3, 1, 4)
    out = {"c_gb": np.ascontiguousarray(gb.reshape(4, 128, 12 * 2 * 256).astype(np.float32))}
    blk = np.arange(4)[None, :, None]
    t = np.arange(16)[None, None, :]
    ds_ = 512 + t - 128 * blk - p[:, None, None]
    gs = rb[:, np.clip(ds_, -128, 128) + 128]
    out["c_gs"] = np.ascontiguousarray(gs.transpose(1, 0, 2, 3).reshape(128, 8 * 4 * 16).astype(np.float32))
    tk = np.arange(16)[:, None]
    tq = np.arange(16)[None, :]
    gn = rb[:, np.clip(tq - tk, -128, 128) + 128]
    out["c_gn"] = np.ascontiguousarray(gn.transpose(1, 0, 2).reshape(16, 128).astype(np.float32))
    return out


def _stripe(a, r):
    sh = a.shape
    return np.ascontiguousarray(a.reshape((64, 256) + sh[1:])[r::4].reshape((4096,) + sh[1:]))


def kernel(x_prompt, x_sample, cache_a_k, cache_a_v, cache_b_k, cache_b_v, cache_mem_k, cache_mem_v,
           mem_prompt, w_in, w_gate, b_gate, rel_bias, lam_qk, subln_g, w_br_a, w_br_b, w_out,
           w_mq, w_mkv, w_mo, norm_g, ffn1_up, ffn1_down, ffn2_up, ffn2_down):
    f = lambda a: np.ascontiguousarray(np.asarray(a, dtype=np.float32))
    nc = _get_nc()
    shared = {
        "w_in": f(w_in[0]), "w_gate": f(w_gate[0]), "b_gate": f(b_gate[0]).reshape(1, 2048),
        "rel_bias": f(rel_bias[0]), "lam_qk": f(lam_qk[0]).reshape(1, 256), "subln_g": f(subln_g[0]).reshape(1, 128),
        "w_br_a": f(w_br_a[0]), "w_br_b": f(w_br_b[0]), "w_out": f(w_out[0]), "w_mq": f(w_mq[0]),
        "w_mkv": f(w_mkv[0]), "w_mo": f(w_mo[0]), "norm_g": f(norm_g[0]),
        "f1u": f(ffn1_up[0]), "f1d": f(ffn1_down[0]), "f2u": f(ffn2_up[0]), "f2d": f(ffn2_down[0]),
    }
    xpn = f(x_prompt)
    xsn = f(x_sample)
    in_maps = []
    for c in range(8):
        n, r = divmod(c, 4)
        m = dict(shared)
        m["xp"] = _stripe(xpn[n], r)
        m["xs"] = xsn[4 * c:4 * c + 4].reshape(NS, D)
        m["cak"] = f(cache_a_k[0, 4 * c:4 * c + 4]).reshape(4, 576, 512)
        m["cav"] = f(cache_a_v[0, 4 * c:4 * c + 4]).reshape(4, 576, 512)
        m["cbk"] = f(cache_b_k[0, 4 * c:4 * c + 4]).reshape(4, 4096, 512)
        m["cbv"] = f(cache_b_v[0, 4 * c:4 * c + 4]).reshape(4, 4096, 512)
        m["cmk"] = f(cache_mem_k[0, 4 * c:4 * c + 4]).reshape(4, 256, 1024)
        m["cmv"] = f(cache_mem_v[0, 4 * c:4 * c + 4]).reshape(4, 256, 1024)
        m["memp"] = f(mem_prompt[n])
        m.update(_consts(r))
        m.update(_relbias_tables(shared["rel_bias"], r))
        in_maps.append(m)
    ncr = int(os.environ.get("K_NCORES", "8"))
    if ncr < 8:
        return run_bass_kernel_spmd(nc, in_maps[:ncr], core_ids=list(range(ncr))).results
    res = run_bass_kernel_spmd(nc, in_maps, core_ids=list(range(8))).results
    g = lambda c, k: np.asarray(res[c][k], dtype=np.float32)
    y_prompt = np.zeros((2, 16384, D), np.float32)
    bkp = np.zeros((2, 16384, 512), np.float32)
    bvp = np.zeros((2, 16384, 512), np.float32)
    for c in range(8):
        n, r = divmod(c, 4)
        y_prompt[n].reshape(64, 256, D)[r::4] = g(c, "y_p").reshape(16, 256, D)
        bkp[n].reshape(64, 256, 512)[r::4] = g(c, "bk_p").reshape(16, 256, 512)
        bvp[n].reshape(64, 256, 512)[r::4] = g(c, "bv_p").reshape(16, 256, 512)
    y_sample = np.concatenate([g(c, "y_s").reshape(4, 16, D) for c in range(8)], 0)
    akp = np.stack([np.concatenate([g(4 * n + 1, "ak_last")[192:256], g(4 * n + 2, "ak_last"), g(4 * n + 3, "ak_last")], 0)
                    for n in range(2)], 0)
    avp = np.stack([np.concatenate([g(4 * n + 1, "av_last")[192:256], g(4 * n + 2, "av_last"), g(4 * n + 3, "av_last")], 0)
                    for n in range(2)], 0)
    mkp = np.stack([g(4 * n, "mk_p") for n in range(2)], 0)
    mvp = np.stack([g(4 * n, "mv_p") for n in range(2)], 0)
    aks = np.concatenate([g(c, "ak_s") for c in range(8)], 0)
    avs = np.concatenate([g(c, "av_s") for c in range(8)], 0)
    bks = np.concatenate([g(c, "bk_s").reshape(4, 16, 512) for c in range(8)], 0)
    bvs = np.concatenate([g(c, "bv_s").reshape(4, 16, 512) for c in range(8)], 0)
    return (y_prompt, y_sample,
            akp.reshape(1, 2, 576, 8, 64), avp.reshape(1, 2, 576, 8, 64),
            bkp.reshape(1, 2, 16384, 4, 2, 64), bvp.reshape(1, 2, 16384, 4, 128),
            mkp.reshape(1, 2, 256, 4, 256), mvp.reshape(1, 2, 256, 4, 256),
            aks.reshape(1, 32, 576, 8, 64), avs.reshape(1, 32, 576, 8, 64),
            bks.reshape(1, 32, 16, 4, 2, 64), bvs.reshape(1, 32, 16, 4, 128))
```

```python
import os
import numpy as np
import ml_dtypes
from contextlib import ExitStack
import concourse.bass as bass
import concourse.mybir as mybir
from concourse.bass_utils import run_bass_kernel_spmd

F32 = mybir.dt.float32
BF16 = mybir.dt.bfloat16
AF = mybir.ActivationFunctionType
ALU = mybir.AluOpType
AX = mybir.AxisListType

ENGS = ["pe", "act", "dve", "pool", "sp"]
KDMA = 8


class Tl:
    __slots__ = ("ap", "lw", "rd", "rd_dma", "name")

    def __init__(self, ap, name=""):
        self.ap = ap
        self.lw = None
        self.rd = {}
        self.rd_dma = []
        self.name = name

    def __getitem__(self, k):
        return self.ap[k]


class Op:
    __slots__ = ("eng", "fn", "waits", "need_sig", "dma", "semkey", "semval", "cc")

    def __init__(self, eng, fn, dma=False, cc=False):
        self.eng = eng
        self.fn = fn
        self.waits = []
        self.need_sig = False
        self.dma = dma
        self.cc = cc
        self.semkey = None
        self.semval = None


class Prog:
    def __init__(self):
        self.q = {e: [] for e in ENGS}
        self.out_dmas = []

    def add(self, eng, fn, reads=(), writes=(), dma=False, cc=False, out=False):
        op = Op(eng, fn, dma, cc)
        deps = []
        for t in reads:
            if t.lw is not None:
                deps.append((t.lw, "raw"))
            if t.name.startswith("pf") or t.name.startswith("pb"):
                for e2, r in t.rd.items():
                    if e2 != eng:
                        deps.append((r, "rar"))
        for t in writes:
            if t.lw is not None:
                deps.append((t.lw, "waw"))
            for r in t.rd.values():
                deps.append((r, "war"))
            for r in t.rd_dma:
                deps.append((r, "war"))
        async_op = dma or cc
        seen = set()
        for d, kind in deps:
            if d is op or id(d) in seen:
                continue
            d_async = d.dma or d.cc
            if (not d_async) and (not async_op) and d.eng == eng:
                if eng == "pe":
                    continue
            seen.add(id(d))
            d.need_sig = True
            op.waits.append(d)
        for t in reads:
            if async_op:
                t.rd_dma.append(op)
                if len(t.rd_dma) > 24:
                    t.rd_dma = t.rd_dma[-24:]
            else:
                t.rd[eng] = op
        for t in writes:
            t.lw = op
            t.rd = {}
            t.rd_dma = []
        self.q[eng].append(op)
        if out:
            self.out_dmas.append(op)
        return op

    def barrier(self, mk, extra={}):
        marks = []
        for e in ENGS:
            t, fn = mk(e, 0)
            pend = [o for o in self.q[e] if (o.dma or o.cc)]
            op = self.add(e, fn, reads=(), writes=(t,) + tuple(extra.get(e, ())), dma=(e == "sp"))
            for o in pend[-(KDMA + 2):]:
                if o not in op.waits:
                    o.need_sig = True
                    op.waits.append(o)
            marks.append(t)
        for e in ENGS:
            t, fn = mk(e, 1)
            self.add(e, fn, reads=marks, writes=(t,) + tuple(extra.get(e, ())), dma=(e == "sp"))

    def emit(self, nc, stack):
        sems = {}
        for e in ENGS:
            sems[e] = stack.enter_context(nc.semaphore("s_" + e))
            sems[("cc", e)] = stack.enter_context(nc.semaphore("c_" + e))
            for j in range(KDMA):
                sems[(e, j)] = stack.enter_context(nc.semaphore("d_%s%d" % (e, j)))
        for e in ENGS:
            cnt = 0
            nd = 0
            ncc = 0
            dmas = []
            for op in self.q[e]:
                if op.cc:
                    ncc += 1
                    op.semkey = ("cc", e)
                    op.semval = ncc
                elif op.dma:
                    op.semkey = (e, nd % KDMA)
                    op.semval = 16 * (nd // KDMA + 1)
                    if nd >= KDMA:
                        op.waits.append(dmas[nd - KDMA])
                    dmas.append(op)
                    nd += 1
                elif op.need_sig:
                    cnt += 1
                    op.semkey = e
                    op.semval = cnt
        block = stack.enter_context(nc.Block())
        prog = self

        def body(e):
            def run(engine):
                known = {}
                for op in prog.q[e]:
                    for d in op.waits:
                        k, v = d.semkey, d.semval
                        if known.get(k, 0) < v:
                            engine.wait_ge(sems[k], v)
                            known[k] = v
                    ins = op.fn(engine)
                    if op.cc:
                        ins.then_inc(sems[op.semkey], 1)
                    elif op.dma:
                        ins.then_inc(sems[op.semkey], 16)
                    elif op.need_sig:
                        ins.then_inc(sems[e], 1)
                if e == "sp":
                    for d in prog.out_dmas:
                        k, v = d.semkey, d.semval
                        if known.get(k, 0) < v:
                            engine.wait_ge(sems[k], v)
                            known[k] = v
            return run

        block.tensor(body("pe"))
        block.scalar(body("act"))
        block.vector(body("dve"))
        block.gpsimd(body("pool"))
        block.sync(body("sp"))


D = 1024
FF = 2816
NPT = 4096
NS = 64
NTOK = NPT + NS
EPS = 1e-6
SLOPES = [2.0 ** (-8.0 * (h + 1) / 4) for h in range(4)]
NEG = -1.0e30
LAM_INIT = 0.8 - 0.6
STAGE = 3


def _mm(out, lhsT, rhs, st, sp):
    return lambda e: e.matmul(out, lhsT=lhsT, rhs=rhs, start=st, stop=sp)


def _mmx(out, lhsT, rhs, st):
    return lambda e: e.matmul(out, lhsT=lhsT, rhs=rhs, start=st, stop=False, skip_group_check=True)


def _tp(out, in_, ident):
    return lambda e: e.transpose(out=out, in_=in_, identity=ident)


def _dma(out, in_):
    return lambda e: e.dma_start(out=out, in_=in_)


def _act(out, in_, func, bias=None, scale=None, accum=None):
    kw = {}
    if bias is not None:
        kw["bias"] = bias
    if scale is not None:
        kw["scale"] = scale
    if accum is not None:
        kw["accum_out"] = accum
    return lambda e: e.activation(out=out, in_=in_, func=func, **kw)


def _copy(out, in_):
    return lambda e: e.tensor_copy(out=out, in_=in_)


def _ts(out, in0, s1, s2, op0, op1=None):
    if op1 is None:
        return lambda e: e.tensor_scalar(out=out, in0=in0, scalar1=s1, scalar2=None, op0=op0)
    return lambda e: e.tensor_scalar(out=out, in0=in0, scalar1=s1, scalar2=s2, op0=op0, op1=op1)


def _stt(out, in0, scalar, in1, op0, op1):
    return lambda e: e.scalar_tensor_tensor(out=out, in0=in0, scalar=scalar, in1=in1, op0=op0, op1=op1)


def _tt(out, in0, in1, op):
    return lambda e: e.tensor_tensor(out=out, in0=in0, in1=in1, op=op)


def _memset(ap, v):
    return lambda e: e.memset(ap, v)


class K:
    def __init__(self):
        self.nc = bass.Bass("TRN2", target_bir_lowering=False)
        self.P = Prog()
        self.dram = {}

    def din(self, name, shape, dt=F32):
        t = self.nc.dram_tensor(name, list(shape), dt, kind="ExternalInput")
        self.dram[name] = t
        return t.ap()

    def dout(self, name, shape, dt=F32):
        t = self.nc.dram_tensor(name, list(shape), dt, kind="ExternalOutput")
        self.dram[name] = t
        return t.ap()

    def dint(self, name, shape, dt):
        t = self.nc.dram_tensor(name, list(shape), dt)
        self.dram[name] = t
        return t.ap()

    def sb(self, st, name, shape, dt):
        return Tl(st.enter_context(self.nc.sbuf_tensor(name, list(shape), dt)), name)

    def rms_T(self, src, rows, TB, gain, dstT, col0=0):
        P = self.P
        ss = self.ss
        P.add("dve", _memset(ss[:rows, 0:TB], 0.0), writes=[ss])
        for tb in range(TB):
            P.add("act", _act(self.junk[:rows, :], src[:rows, tb, :], AF.Square, accum=ss[:rows, tb:tb + 1]),
                  reads=[src, ss], writes=[self.junk, ss])
        self.rstd(ss, rows, TB, D * EPS)
        for tb in range(TB):
            xn = self.xn[tb % 2]
            P.add("dve", _stt(xn[:rows, :], src[:rows, tb, :], self.rs[:rows, tb:tb + 1], gain[:rows, :],
                              ALU.mult, ALU.mult), reads=[src, self.rs, gain], writes=[xn])
            pb = self.pb[tb % 2]
            for kc in range(8):
                P.add("pe", _tp(pb[:, kc * 128:kc * 128 + rows], xn[:rows, kc * 128:(kc + 1) * 128],
                                self.identb[:rows, :rows]), reads=[xn, self.identb], writes=[pb])
            c0 = col0 + tb * rows
            eng = "act" if tb % 2 == 0 else "dve"
            src_ap = pb[:, 0:1024].rearrange("p (k c) -> p k c", k=8)[:, :, 0:rows]
            if eng == "act":
                P.add("act", _act(dstT[:, :, c0:c0 + rows], src_ap, AF.Copy), reads=[pb], writes=[dstT])
            else:
                P.add("dve", _copy(dstT[:, :, c0:c0 + rows], src_ap), reads=[pb], writes=[dstT])


    def rstd(self, ss, rows, n, eps_tot):
        P = self.P
        P.add("dve", _ts(self.rs[:rows, 0:n], ss[:rows, 0:n], float(eps_tot), None, ALU.add), reads=[ss], writes=[self.rs])
        P.add("act", _act(self.rs[:rows, 0:n], self.rs[:rows, 0:n], AF.Sqrt), reads=[self.rs], writes=[self.rs])
        P.add("dve", lambda e: e.reciprocal(out=self.rs[:rows, 0:n], in_=self.rs[:rows, 0:n]), reads=[self.rs], writes=[self.rs])

    def load_w(self, wb, view, src, k):
        tls = self.wtl.get(src.tensor.name, [])
        self.P.add("pool", _dma(view, src.rearrange("(k p) c -> p k c", p=128)), reads=tls, writes=[wb], dma=True)

    def convert_w(self, name):
        f32, bf = self.wf32[name]
        R, C = f32.shape
        tls = self.wtl[bf.tensor.name]
        for r0 in range(0, R, 128):
            for c0 in range(0, C, 2048):
                c1 = min(C, c0 + 2048)
                t = Tl(None, "wc")
                tls.append(t)
                self.P.add("pool", _dma(bf[r0:r0 + 128, c0:c1], f32[r0:r0 + 128, c0:c1]), writes=[t], dma=True)

    def swiglu_ffn(self, xT, ntok, rows, TB, w_up, w_down, y_t):
        P = self.P
        aT = self.aT
        wi = 0
        pair = 0
        for g in range(4):
            npair = 6 if g < 3 else 4
            wb = self.wbuf[self.wcnt % 2]
            self.wcnt += 1
            wv = wb[:, 0:12288].rearrange("p (k c) -> p k c", k=8)
            c0 = g * 768
            nc_ = npair * 128
            self.load_w(wb, wv[:, :, 0:nc_], w_up[:, c0:c0 + nc_], 8)
            self.load_w(wb, wv[:, :, 768:768 + nc_], w_up[:, FF + c0:FF + c0 + nc_], 8)
            for pi in range(npair):
                pg = self.pf[(pair % 2) * 2]
                pu = self.pf[(pair % 2) * 2 + 1]
                for kc in range(8):
                    P.add("pe", _mm(pg[:, 0:ntok], wv[:, kc, pi * 128:(pi + 1) * 128], xT[:, kc, 0:ntok], kc == 0, kc == 7),
                          reads=[wb, xT], writes=[pg])
                for kc in range(8):
                    P.add("pe", _mm(pu[:, 0:ntok], wv[:, kc, 768 + pi * 128:768 + (pi + 1) * 128], xT[:, kc, 0:ntok],
                                    kc == 0, kc == 7), reads=[wb, xT], writes=[pu])
                sg = self.sg[pair % 2]
                P.add("act", _act(sg[:, 0:ntok], pg[:, 0:ntok], AF.Silu), reads=[pg], writes=[sg])
                P.add("dve", _tt(aT[:, pair, 0:ntok], sg[:, 0:ntok], pu[:, 0:ntok], ALU.mult),
                      reads=[sg, pu], writes=[aT])
                pair += 1
        for half in range(2):
            wb = self.wbuf[self.wcnt % 2]
            self.wcnt += 1
            wv = wb[:, 0:11264].rearrange("p (k c) -> p k c", k=22)
            self.load_w(wb, wv, w_down[:, half * 512:(half + 1) * 512], 22)
            for tb in range(TB):
                ps = self.pf[4 + (tb % 2)]
                for fc in range(22):
                    P.add("pe", _mm(ps[:rows, :], aT[:, fc, tb * rows:(tb + 1) * rows], wv[:, fc, :], fc == 0, fc == 21),
                          reads=[aT, wb], writes=[ps])
                P.add("act", _act(y_t[:rows, tb, half * 512:(half + 1) * 512], ps[:rows, :], AF.Copy),
                      reads=[ps], writes=[y_t])

    def resid_norm(self, h_t, y_t, rows, TB, gain, scale_in_gain=True):
        P = self.P
        ss = self.ss
        P.add("dve", _memset(ss[:rows, 0:TB], 0.0), writes=[ss])
        for tb in range(TB):
            P.add("act", _act(self.junk[:rows, :], y_t[:rows, tb, :], AF.Square, accum=ss[:rows, tb:tb + 1]),
                  reads=[y_t, ss], writes=[self.junk, ss])
        self.rstd(ss, rows, TB, D * EPS)
        for tb in range(TB):
            P.add("dve", _stt(y_t[:rows, tb, :], y_t[:rows, tb, :], self.rs[:rows, tb:tb + 1], gain[:rows, :],
                              ALU.mult, ALU.mult), reads=[y_t, self.rs, gain], writes=[y_t])
            P.add("dve", _tt(h_t[:rows, tb, :], h_t[:rows, tb, :], y_t[:rows, tb, :], ALU.add),
                  reads=[y_t, h_t], writes=[h_t])

    def load_gain(self, g_tl, idx, coef):
        P = self.P
        src = bass.AP(self.norm_g.tensor, idx * D, [[0, 128], [1, D]])
        P.add("sp", _dma(g_tl[:, :], src), writes=[g_tl], dma=True)
        P.add("dve", _ts(g_tl[:, :], g_tl[:, :], float(coef), None, ALU.mult), reads=[g_tl], writes=[g_tl])


    def mem_kv(self, memp, g6, w_mkv, mk_p, mv_p, x_t, xT, kvst, qst, mkT_scr, t_mkT, mv_scr, t_mv):
        P = self.P
        P.add("sp", _dma(x_t[:, 0:2, :], memp.rearrange("(tb p) d -> p tb d", p=128)), writes=[x_t], dma=True)
        self.rms_T(x_t, 128, 2, g6, xT)
        for grp in range(2):
            wb = self.wbuf[self.wcnt % 2]
            self.wcnt += 1
            wv = wb[:, 0:12288].rearrange("p (k c) -> p k c", k=8)
            self.load_w(wb, wv[:, :, 0:1024], w_mkv[:, grp * 1024:(grp + 1) * 1024], 8)
            for tb in range(2):
                for half in range(2):
                    ps = self.pf[4 + half]
                    for kc in range(8):
                        P.add("pe", _mm(ps[:, :], xT[:, kc, tb * 128:(tb + 1) * 128], wv[:, kc, half * 512:(half + 1) * 512],
                                        kc == 0, kc == 7), reads=[wb, xT], writes=[ps])
                    P.add("dve", _copy(kvst[:, tb * 2 + half, :], ps[:, :]), reads=[ps], writes=[kvst])
            outt = mk_p if grp == 0 else mv_p
            for tb in range(2):
                P.add("sp", _dma(outt[tb * 128:(tb + 1) * 128, :].rearrange("p (hf d) -> p hf d", hf=2),
                                 kvst[:, 2 * tb:2 * tb + 2, :]), reads=[kvst], dma=True, out=True)
            if grp == 1:
                P.add("act", _act(qst[:, :, :], kvst[:, :, :], AF.Copy), reads=[kvst], writes=[qst])
                for tb in range(2):
                    P.add("sp", _dma(mv_scr[tb * 128:(tb + 1) * 128, :].rearrange("p (hf d) -> p hf d", hf=2),
                                     qst[:, 2 * tb:2 * tb + 2, :]), reads=[qst], writes=[t_mv], dma=True)
            else:
                for c in range(8):
                    ps = self.pf[c % 4]
                    for kc in range(8):
                        P.add("pe", _mm(ps[:, 0:256], wv[:, kc, c * 128:(c + 1) * 128], xT[:, kc, 0:256], kc == 0, kc == 7),
                              reads=[wb, xT], writes=[ps])
                    P.add("act", _act(self.junk[:, 0:256], ps[:, 0:256], AF.Copy), reads=[ps], writes=[self.junk])
                    P.add("sp", _dma(mkT_scr[c, :, :], self.junk[:, 0:256]), reads=[self.junk], writes=[t_mkT], dma=True)


    def exchange(self, si, L):
        P = self.P
        RG = [[0, 1, 2, 3], [4, 5, 6, 7]]
        for nm in ["kbT", "vb", "kaT", "va"]:
            a_in, t_in_list = L[nm + "_in"], L["t_" + nm + "_in"][si]
            a_g, t_g = L[nm + "_g"], L["t_" + nm + "_g"][si]
            src = a_in[si * 1024:(si + 1) * 1024, :]
            dst = a_g[si * 4096:(si + 1) * 4096, :]
            if os.environ.get("K_NOAG"):
                for sr in range(4):
                    P.add("sp", _dma(dst[sr * 1024:(sr + 1) * 1024, :], src), reads=t_in_list, writes=[t_g], dma=True)
                continue
            P.add("pool", (lambda a, b: (lambda e: e.collective_compute("AllGather", ALU.bypass, replica_groups=RG,
                                                                         ins=[a.opt()], outs=[b.opt()])))(src, dst),
                  reads=t_in_list, writes=[t_g], cc=True)

    def phase23(self, top, L):
        self.phase2(top, L)
        self.phase3(top, L)

    def phase2(self, top, L):
        P = self.P
        for nm in ["w_br_a", "w_br_b", "w_out", "w_mq", "w_mo", "f2u", "f2d"]:
            self.convert_w(nm)
        nc = self.nc
        pf, pb = self.pf, self.pb
        identb = self.identb
        kbT_g, vb_g, kaT_g, va_g = L["kbT_g"], L["vb_g"], L["kaT_g"], L["va_g"]
        t_kbT_g, t_vb_g, t_kaT_g, t_va_g = L["t_kbT_g"], L["t_vb_g"], L["t_kaT_g"], L["t_va_g"]
        qa_scr, qb_scr, yT_scr = L["qa_scr"], L["qb_scr"], L["yT_scr"]
        t_qa, t_qb, t_yT = L["t_qa"], L["t_qb"], L["t_yT"]
        p2 = ExitStack()
        with p2:
            s2 = lambda name, shape, dt: self.sb(p2, name, shape, dt)
            btab = s2("btab", [128, 4, 128], F32)
            dtab_tl = self.wbuf[0]
            dtab = dtab_tl[:, 0:8192].rearrange("p (h i q) -> p h i q", h=4, i=8)
            btabs = s2("btabs", [128, 4, 33], F32)
            dtabs = s2("dtabs", [16, 4, 16], BF16)
            maska = s2("maska", [128, 12, 256], F32)
            lamt = s2("lamt", [128, 256], F32)
            lamp = s2("lamp", [128, 128], F32)
            lamv = s2("lamv", [128, 8], F32)
            sg8 = s2("sg8", [128, 128], F32)
            qbt = [[s2("qbt%d_%d" % (i, c), [128, 4, 256], BF16) for c in range(2)] for i in range(2)]
            kt = [s2("kt%d" % i, [128, 2, 256], BF16) for i in range(4)]
            vt = [s2("vt%d" % i, [128, 2, 2, 130], BF16) for i in range(4)]
            pt = [s2("pt%d" % i, [128, 512], BF16) for i in range(4)]
            sbs = [s2("sbs%d" % i, [128, 512], F32) for i in range(3)]
            ep_r = s2("ep_r", [128, 8], F32)
            ep_t = [s2("ep_t%d" % i, [128, 128], F32) for i in range(2)]
            ep_y = [s2("ep_y%d" % i, [128, 128], F32) for i in range(2)]
            yb = [s2("yb%d" % i, [128, 2, 256], BF16) for i in range(2)]
            ybT = [s2("ybT%d" % i, [128, 2, 256], BF16) for i in range(2)]
            gt = s2("gt", [128, 12, 2, 256], F32)
            qat = [[s2("qat%d_%d" % (i, c), [128, 256], BF16) for c in range(2)] for i in range(2)]
            kat = [s2("kat%d" % i, [128, 2, 128], BF16) for i in range(4)]
            vat = [s2("vat%d" % i, [128, 2, 2, 66], BF16) for i in range(4)]
            ya = [s2("ya%d" % i, [128, 2, 128], BF16) for i in range(2)]
            yaT = [s2("yaT%d" % i, [128, 256], BF16) for i in range(2)]

            P.add("sp", _dma(btab[:, :, :], L["c_btab"].rearrange("p (h o) -> p h o", h=4)), writes=[btab], dma=True)
            P.add("sp", _dma(dtab_tl[:, 0:8192], L["c_dtab"][:, :]), writes=[dtab_tl], dma=True)
            P.add("sp", _dma(btabs[:, :, :], L["c_btabs"].rearrange("p (h o) -> p h o", h=4)), writes=[btabs], dma=True)
            P.add("sp", _dma(dtabs[:, :, :], L["c_dtabs"].rearrange("p (h o) -> p h o", h=4)), writes=[dtabs], dma=True)
            P.add("sp", _dma(maska[:, :, :], L["c_maska"].rearrange("p (i q) -> p i q", i=12)), writes=[maska], dma=True)
            for pair in qbt:
                for t_ in pair:
                    P.add("dve", _memset(t_[:, :, :], 0.0), writes=[t_])
            for pair in qat:
                for t_ in pair:
                    P.add("dve", _memset(t_[:, :], 0.0), writes=[t_])
            for v in vt:
                P.add("dve", _memset(v[:, :, :, 128:130], 1.0), writes=[v])
            for v in vat:
                P.add("dve", _memset(v[:, :, :, 64:66], 1.0), writes=[v])
            vt_b = [[Tl(v.ap, v.name + "_b%d" % bb) for bb in range(2)] for v in vt]
            vat_b = [[Tl(v.ap, v.name + "_b%d" % bb) for bb in range(2)] for v in vat]
            for v, vb2 in list(zip(vt, vt_b)) + list(zip(vat, vat_b)):
                for t_ in vb2:
                    t_.lw = v.lw
            P.add("sp", _dma(lamt[:, :], bass.AP(L["lam_qk"].tensor, 0, [[0, 128], [1, 256]])), writes=[lamt], dma=True)
            P.add("dve", _tt(lamp[:, 0:64], lamt[:, 0:64], lamt[:, 64:128], ALU.mult), reads=[lamt], writes=[lamp])
            P.add("dve", _tt(lamp[:, 64:128], lamt[:, 128:192], lamt[:, 192:256], ALU.mult), reads=[lamt], writes=[lamp])
            P.add("dve", lambda e: e.reduce_sum(out=lamv[:, 0:1], in_=lamp[:, 0:64], axis=AX.X), reads=[lamp], writes=[lamv])
            P.add("dve", lambda e: e.reduce_sum(out=lamv[:, 1:2], in_=lamp[:, 64:128], axis=AX.X), reads=[lamp], writes=[lamv])
            P.add("act", _act(lamv[:, 2:4], lamv[:, 0:2], AF.Exp), reads=[lamv], writes=[lamv])
            P.add("dve", _tt(lamv[:, 4:5], lamv[:, 3:4], lamv[:, 2:3], ALU.subtract), reads=[lamv], writes=[lamv])
            P.add("dve", _ts(lamv[:, 5:6], lamv[:, 4:5], -LAM_INIT, None, ALU.add), reads=[lamv], writes=[lamv])
            nlam = lamv[:, 5:6]
            P.add("sp", _dma(sg8[:, :], bass.AP(L["subln_g"].tensor, 0, [[0, 128], [1, 128]])), writes=[sg8], dma=True)
            P.add("dve", _ts(sg8[:, :], sg8[:, :], float((1.0 - LAM_INIT) * np.sqrt(128.0)), None, ALU.mult), reads=[sg8], writes=[sg8])

            def acc(idx, rows, ncol=130):
                per = 512 // ncol
                t = pf[2 + idx // per]
                c0 = (idx % per) * ncol
                return t, t[:rows, c0:c0 + ncol]

            cnt = {"s": 0, "k": 0, "pt": 0, "ep": 0, "yb": 0, "q": 0, "sb": 0}
            opened = set()

            def mm2(idx, rows, ncol, lhsT, rhs, first, reads):
                ta, aa = acc(idx, rows, ncol)
                st = False
                if first and ta.name not in opened:
                    opened.add(ta.name)
                    st = True
                P.add("pe", _mmx(aa, lhsT, rhs, st), reads=reads, writes=[ta])

            def diff_epilogue(rows, hh_list, n_qs, dst_fn, i0=lambda hh, qs: hh * 4 + qs, i1=lambda hh, qs: hh * 4 + 2 + qs, getacc=None):
                if getacc is None:
                    getacc = acc
                for hh in hh_list:
                    for qs in range(n_qs):
                        t0, a0 = getacc(i0(hh, qs), rows)
                        t1, a1 = getacc(i1(hh, qs), rows)
                        et = ep_t[cnt["ep"] % 2]
                        ey = ep_y[cnt["ep"] % 2]
                        cnt["ep"] += 1
                        P.add("dve", lambda e, a=a0: e.reciprocal(out=ep_r[:rows, 0:1], in_=a[:, 128:129]), reads=[t0], writes=[ep_r])
                        P.add("dve", lambda e, a=a1: e.reciprocal(out=ep_r[:rows, 1:2], in_=a[:, 128:129]), reads=[t1, ep_r], writes=[ep_r])
                        P.add("dve", _tt(ep_r[:rows, 2:3], ep_r[:rows, 1:2], nlam[:rows, :], ALU.mult), reads=[ep_r, lamv], writes=[ep_r])
                        P.add("dve", _ts(et[:rows, :], a0[:, 0:128], ep_r[:rows, 0:1], None, ALU.mult), reads=[t0, ep_r], writes=[et])
                        P.add("dve", _stt(ey[:rows, :], a1[:, 0:128], ep_r[:rows, 2:3], et[:rows, :], ALU.mult, ALU.add),
                              reads=[t1, ep_r, et], writes=[ey])
                        P.add("dve", _memset(ep_r[:rows, 3:4], 0.0), reads=[ep_r], writes=[ep_r])
                        P.add("act", _act(et[:rows, :], ey[:rows, :], AF.Square, accum=ep_r[:rows, 3:4]), reads=[ey, ep_r], writes=[et, ep_r])
                        P.add("dve", _ts(ep_r[:rows, 4:5], ep_r[:rows, 3:4], 128.0 * EPS, None, ALU.add), reads=[ep_r], writes=[ep_r])
                        P.add("act", _act(ep_r[:rows, 4:5], ep_r[:rows, 4:5], AF.Sqrt), reads=[ep_r], writes=[ep_r])
                        P.add("dve", lambda e: e.reciprocal(out=ep_r[:rows, 5:6], in_=ep_r[:rows, 4:5]), reads=[ep_r], writes=[ep_r])
                        dt_, dap = dst_fn(hh, qs)
                        P.add("dve", _stt(dap, ey[:rows, :], ep_r[:rows, 5:6], sg8[:rows, :], ALU.mult, ALU.mult),
                              reads=[ey, ep_r, sg8], writes=[dt_])

            class Pipe:
                def __init__(self, depth=1):
                    self.depth = depth
                    self.pending = []
                    self.later = []

                def push(self, s1, s2):
                    s1()
                    self.pending.append(s2)
                    if len(self.pending) > self.depth:
                        self.pending.pop(0)()
                    for it in list(self.later):
                        it[0] -= 1
                        if it[0] <= 0:
                            self.later.remove(it)
                            it[1]()

                def flush(self):
                    while self.pending:
                        self.pending.pop(0)()

                def flush_later(self):
                    for it in self.later:
                        it[1]()
                    self.later = []

            pipe = Pipe(2)
            sbankB = [pf[0], pf[1], pf[5]]
            accs = s2("accs", [128, 8, 130], F32)

            def getaccs(idx, rows, ncol=130):
                return accs, accs[:rows, idx, :]

            NJ = int(os.environ.get("K_J", "16"))
            for j in range(NJ if not os.environ.get("K_NOB") else 0):
                qb_ = qbt[j % 2]
                for c in range(2):
                    P.add("sp", _dma(qb_[c][c * 64:(c + 1) * 64, :, :], qb_scr[:, c * 64:(c + 1) * 64, j * 256:(j + 1) * 256].rearrange("c p t -> p c t")),
                          reads=[t_qb], writes=[qb_[c]], dma=True)
                for hg in range(2):
                    nkp = 4 * j + 4
                    for kbp in range(nkp):
                        rr, jj = kbp % 4, kbp // 4
                        k_ = kt[cnt["k"] % 4]
                        v_ = vt[cnt["k"] % 4]
                        vb2 = vt_b[cnt["k"] % 4]
                        cnt["k"] += 1
                        row0 = (jj // 4) * 4096 + rr * 1024 + (jj % 4) * 256
                        P.add("sp", _dma(k_[:, :, :], kbT_g[row0:row0 + 256, hg * 256:(hg + 1) * 256].rearrange("(b p) x -> p b x", p=128)),
                              reads=[t_kbT_g[jj // 4]], writes=[k_], dma=True)
                        for blk in range(2):
                            P.add("sp", _dma(v_[:, blk, :, 0:128],
                                             vb_g[row0 + blk * 128:row0 + (blk + 1) * 128, hg * 256:(hg + 1) * 256].rearrange("p (hh e) -> p hh e", hh=2)),
                                  reads=[t_vb_g[jj // 4]], writes=[vb2[blk]], dma=True)
                        diag = kbp >= 4 * j
                        for blk in range(2):
                            i_d = 2 * (kbp - 4 * j) + blk
                            oi = 2 * kbp + blk - 8 * j + 120
                            for hh in range(2):
                                h = 2 * hg + hh
                                ps = sbankB[cnt["s"] % 3]
                                cnt["s"] += 1
                                p_ = pt[cnt["pt"] % 4]
                                cnt["pt"] += 1
                                first = (kbp == 0 and blk == 0)

                                def s1(ps=ps, k_=k_, blk=blk, hh=hh, h=h, qb_=qb_, diag=diag, i_d=i_d):
                                    for c in range(2):
                                        P.add("pe", _mmx(ps[:, c * 256:(c + 1) * 256], k_[:, blk, hh * 128:(hh + 1) * 128],
                                                         qb_[c][:, h, :], c == 0), reads=[k_, qb_[c]], writes=[ps])
                                    if diag:
                                        for c in range(2):
                                            P.add("pe", _mmx(ps[:, c * 256:(c + 1) * 256], identb[:, :], dtab[:, h, i_d, :], False),
                                                  reads=[identb, dtab_tl], writes=[ps])

                                def s2_(ps=ps, p_=p_, v_=v_, vtl=vb2[blk], blk=blk, hh=hh, h=h, oi=oi, first=first):
                                    P.add("act", _act(p_[:, :], ps[:, :], AF.Exp, bias=btab[:, h, oi:oi + 1], scale=0.125),
                                          reads=[ps, btab], writes=[p_])
                                    if first and hh == 0:
                                        opened.clear()
                                    for c in range(2):
                                        for qs in range(2):
                                            mm2(hh * 4 + c * 2 + qs, 128, 130, p_[:, c * 256 + qs * 128:c * 256 + (qs + 1) * 128], v_[:, blk, hh, :],
                                                first, [p_, vtl])

                                pipe.push(s1, s2_)
                    pipe.flush()
                    y_ = yb[cnt["yb"] % 2]
                    yT_ = ybT[cnt["yb"] % 2]
                    cnt["yb"] += 1
                    P.add("dve", _copy(accs[:, 0:3, :], pf[2][:, 0:390].rearrange("p (a b) -> p a b", a=3)), reads=[pf[2]], writes=[accs])
                    P.add("act", _act(accs[:, 3:6, :], pf[3][:, 0:390].rearrange("p (a b) -> p a b", a=3), AF.Copy), reads=[pf[3]], writes=[accs])
                    P.add("dve", _copy(accs[:, 6:8, :], pf[4][:, 0:260].rearrange("p (a b) -> p a b", a=2)), reads=[pf[4]], writes=[accs])
                    diff_epilogue(128, [0, 1], 2, lambda hh, qs, y_=y_: (y_, y_[:, qs, hh * 128:(hh + 1) * 128]), getacc=getaccs)

                    def tpose(y_=y_, yT_=yT_, hg=hg, j=j):
                        pbt = pb[hg % 2]
                        for hh in range(2):
                            for qs in range(2):
                                P.add("pe", _tp(pbt[:, hh * 256 + qs * 128:hh * 256 + (qs + 1) * 128], y_[:, qs, hh * 128:(hh + 1) * 128], identb[:, :]),
                                      reads=[y_, identb], writes=[pbt])
                        P.add("act", _act(yT_[:, :, :], pbt[:, 0:512].rearrange("p (a b) -> p a b", a=2), AF.Copy), reads=[pbt], writes=[yT_])
                        P.add("sp", _dma(yT_scr[4 + 2 * hg:6 + 2 * hg, :, j * 256:(j + 1) * 256].rearrange("c p t -> p c t"), yT_[:, :, :]),
                              reads=[yT_], writes=[t_yT], dma=True)

                    pipe.later.append([12, tpose])
            pipe.flush()
            pipe.flush_later()

            pipeA = Pipe(2)
            sbank = [pf[0], pf[1], pf[5]]
            for hp in range(4 if not os.environ.get("K_NOA") else 0):
                P.add("sp", _dma(gt[:, :, :, :].rearrange("p i h q -> p (i h q)"), L["c_gb"][hp, :, :]), writes=[gt], dma=True)
                for hh in range(2):
                    P.add("dve", _tt(gt[:, :, hh, :], gt[:, :, hh, :], maska[:, :, :], ALU.add), reads=[gt, maska], writes=[gt])
                for j in range(NJ):
                    qa_ = qat[j % 2]
                    for hh in range(2):
                        P.add("sp", _dma(qa_[hh][hh * 64:(hh + 1) * 64, :], qa_scr[hp, hh * 64:(hh + 1) * 64, j * 256:(j + 1) * 256]),
                              reads=[t_qa], writes=[qa_[hh]], dma=True)
                    ilist = [i for i in range(12) if 8 * j - 4 + i >= 0]
                    for ip in range(ilist[0] // 2, 6):
                        tt_ = 4 * j - 2 + ip
                        rr, jj = tt_ % 4, tt_ // 4
                        k_ = kat[cnt["k"] % 4]
                        v_ = vat[cnt["k"] % 4]
                        vb2 = vat_b[cnt["k"] % 4]
                        cnt["k"] += 1
                        row0 = (jj // 4) * 4096 + rr * 1024 + (jj % 4) * 256
                        P.add("sp", _dma(k_[:, :, :], kaT_g[row0:row0 + 256, hp * 128:(hp + 1) * 128].rearrange("(b p) x -> p b x", p=128)),
                              reads=[t_kaT_g[jj // 4]], writes=[k_], dma=True)
                        tr0 = row0
                        for blk in range(2):
                            P.add("sp", _dma(v_[:, blk, :, 0:64],
                                             va_g[tr0 + blk * 128:tr0 + (blk + 1) * 128, hp * 128:(hp + 1) * 128].rearrange("p (hh d) -> p hh d", hh=2)),
                                  reads=[t_va_g[jj // 4]], writes=[vb2[blk]], dma=True)
                        for blk in range(2):
                            i = 2 * ip + blk
                            ps = sbank[cnt["s"] % 3]
                            cnt["s"] += 1
                            s_ = sbs[cnt["sb"] % 3]
                            cnt["sb"] += 1
                            p_ = pt[cnt["pt"] % 3]
                            cnt["pt"] += 1
                            first = (ip == ilist[0] // 2 and blk == 0)

                            def s1(ps=ps, k_=k_, blk=blk, qa_=qa_):
                                for hh in range(2):
                                    P.add("pe", _mm(ps[:, hh * 256:(hh + 1) * 256], k_[:, blk, :],
                                                    qa_[hh][:, :], True, True), reads=[k_, qa_[hh]], writes=[ps])

                            def s2_(ps=ps, s_=s_, p_=p_, v_=v_, vtl=vb2[blk], blk=blk, i=i, first=first):
                                P.add("dve", _stt(s_[:, :], ps[:, :], 0.125, gt[:, i, :, :].rearrange("p h q -> p (h q)"), ALU.mult, ALU.add),
                                      reads=[ps, gt], writes=[s_])
                                P.add("act", _act(p_[:, :], s_[:, :], AF.Exp), reads=[s_], writes=[p_])
                                if first:
                                    opened.clear()
                                for hh in range(2):
                                    for qs in range(2):
                                        mm2(hh * 2 + qs, 128, 66, p_[:, hh * 256 + qs * 128:hh * 256 + (qs + 1) * 128], v_[:, blk, hh, :], first, [p_, vtl])

                            pipeA.push(s1, s2_)
                    pipeA.flush()
                    y_ = ya[j % 2]
                    yT_ = yaT[j % 2]
                    P.add("dve", _copy(accs[:, 0:4, 0:66], pf[2][:, 0:264].rearrange("p (a b) -> p a b", a=4)), reads=[pf[2]], writes=[accs])
                    for hh in range(2):
                        for qs in range(2):
                            k = hh * 2 + qs
                            P.add("dve", lambda e, k=k: e.reciprocal(out=ep_r[:, k:k + 1], in_=accs[:, k, 64:65]), reads=[accs], writes=[ep_r])
                            P.add("dve", _ts(y_[:, qs, hh * 64:(hh + 1) * 64], accs[:, k, 0:64], ep_r[:, k:k + 1], None, ALU.mult),
                                  reads=[accs, ep_r], writes=[y_])

                    def tposeA(y_=y_, yT_=yT_, hp=hp, j=j):
                        pbt = pb[j % 2]
                        for qs in range(2):
                            P.add("pe", _tp(pbt[:, qs * 128:(qs + 1) * 128], y_[:, qs, :], identb[:, :]), reads=[y_, identb], writes=[pbt])
                        P.add("act", _act(yT_[:, :], pbt[:, 0:256], AF.Copy), reads=[pbt], writes=[yT_])
                        P.add("sp", _dma(yT_scr[hp, :, j * 256:(j + 1) * 256], yT_[:, :]), reads=[yT_], writes=[t_yT], dma=True)

                    pipeA.later.append([6, tposeA])
            pipeA.flush()
            pipeA.flush_later()

            self.sample_attn(p2, L, locals())
            P.barrier(self.mkbar, {'pe': [self.pf[5]]})

    def sample_attn(self, p2, L, L2):
        P = self.P
        pf, pb, identb = self.pf, self.pb, self.identb
        s2 = lambda name, shape, dt: self.sb(p2, name, shape, dt)
        acc, cnt, diff_epilogue, mm2, opened = L2["acc"], L2["cnt"], L2["diff_epilogue"], L2["mm2"], L2["opened"]
        btabs, dtabs, ep_r, sbs, pt = L2["btabs"], L2["dtabs"], L2["ep_r"], L2["sbs"], L2["pt"]
        cak, cav, cbk, cbv = L["cak"], L["cav"], L["cbk"], L["cbv"]
        qaT_s, kaT_s, qbT_s, kbT_s, va_s, vb_s = L["qaT_s"], L["kaT_s"], L["qbT_s"], L["kbT_s"], L["va_s"], L["vb_s"]
        yT_scr, t_yT = L["yT_scr"], L["t_yT"]
        ckb = [s2("ckb%d" % i, [128, 8, 512], BF16) for i in range(2)]
        cvb = [s2("cvb%d" % i, [128, 8, 4, 130], BF16) for i in range(2)]
        kts = [s2("kts%d" % i, [128, 4, 128], BF16) for i in range(3)]
        cka = s2("cka", [128, 4, 512], BF16)
        cva = s2("cva", [128, 4, 8, 66], BF16)
        gs = s2("gs", [128, 8, 4, 16], F32)
        gn = s2("gn", [16, 8, 16], F32)
        ysb = s2("ysb", [16, 4, 128], BF16)
        ysa = s2("ysa", [16, 512], BF16)
        yT_s = s2("yT_s", [128, 8, NS], BF16)
        P.add("sp", _dma(gs[:, :, :, :].rearrange("p h k t -> p (h k t)"), L["c_gs"][:, :]), writes=[gs], dma=True)
        P.add("sp", _dma(gn[:, :, :].rearrange("p h t -> p (h t)"), L["c_gn"][:, :]), writes=[gn], dma=True)
        cv_bs = []
        for v in cvb:
            P.add("dve", _memset(v[:, :, :, 128:130], 1.0), writes=[v])
            lst = [Tl(v.ap, v.name + "_k%d" % kk) for kk in range(8)]
            for t_ in lst:
                t_.lw = v.lw
            cv_bs.append(lst)
        P.add("dve", _memset(cva[:, :, :, 64:66], 1.0), writes=[cva])
        nb = 4 if not os.environ.get("K_NOS") else 0
        Pipe = L2["Pipe"]
        pipeT = Pipe(1)
        pipeS = Pipe(1)
        sbankS = [pf[0], pf[1], pf[5]]
        for b in range(nb):
            for ch in range(4):
                ck = ckb[ch % 2]
                cv = cvb[ch % 2]
                cv_b = cv_bs[ch % 2]
                P.add("pool", _dma(ck[:, :, :], cbk[b, ch * 1024:(ch + 1) * 1024, :].rearrange("(k p) c -> p k c", p=128)),
                      writes=[ck], dma=True)
                for kk in range(8):
                    r0 = ch * 1024 + kk * 128
                    P.add("pool", _dma(cv[:, kk, :, 0:128], cbv[b, r0:r0 + 128, :].rearrange("p (h e) -> p h e", h=4)),
                          writes=[cv_b[kk]], dma=True)
                for kk in range(8):
                    blk = ch * 8 + kk
                    pbt = pb[blk % 2]
                    k_ = kts[blk % 3]
                    ps = sbankS[cnt["s"] % 3]
                    cnt["s"] += 1
                    p_ = pt[cnt["pt"] % 4]
                    cnt["pt"] += 1

                    def s0(pbt=pbt, ck=ck, kk=kk, k_=k_):
                        for h in range(4):
                            P.add("pe", _tp(pbt[:, h * 128:(h + 1) * 128], ck[:, kk, h * 128:(h + 1) * 128], identb[:, :]),
                                  reads=[ck, identb], writes=[pbt])
                        P.add("dve", _copy(k_[:, :, :], pbt[:, 0:512].rearrange("p (h k) -> p h k", h=4)), reads=[pbt], writes=[k_])

                    def s1(ps=ps, k_=k_, b=b):
                        for h in range(4):
                            for c in range(2):
                                o = (h * 2 + c) * 16
                                P.add("pe", _mm(ps[:, o:o + 16], k_[:, h, :], qbT_s[c][:, h, b * 16:(b + 1) * 16],
                                                True, True), reads=[k_, qbT_s[c]], writes=[ps])

                    def s2_(ps=ps, p_=p_, blk=blk, cv=cv, kk=kk, cvt=cv_b[kk]):
                        for h in range(4):
                            P.add("act", _act(p_[:, h * 32:(h + 1) * 32], ps[:, h * 32:(h + 1) * 32], AF.Exp, bias=btabs[:, h, blk:blk + 1], scale=0.125),
                                  reads=[ps, btabs], writes=[p_])
                        if blk == 0:
                            opened.clear()
                        for h in range(4):
                            for c in range(2):
                                o = (h * 2 + c) * 16
                                mm2(h * 2 + c, 16, 130, p_[:, o:o + 16], cv[:, kk, h, :], blk == 0, [p_, cvt])

                    pipeT.push(s0, (lambda s1=s1, s2_=s2_: pipeS.push(s1, s2_)))
            pipeT.flush()
            pipeS.flush()
            ps = pf[cnt["s"] % 2]
            cnt["s"] += 1
            for h in range(4):
                for c in range(2):
                    o = (h * 2 + c) * 16
                    P.add("pe", _mmx(ps[:16, o:o + 16], kbT_s[:, h, b * 16:(b + 1) * 16],
                                     qbT_s[c][:, h, b * 16:(b + 1) * 16], h == 0 and c == 0), reads=[kbT_s, qbT_s[c]], writes=[ps])
                    P.add("pe", _mmx(ps[:16, o:o + 16], identb[0:16, 0:16], dtabs[0:16, h, :], False), reads=[identb, dtabs], writes=[ps])
            p_ = pt[cnt["pt"] % 3]
            cnt["pt"] += 1
            P.add("act", _act(p_[:16, 0:128], ps[:16, 0:128], AF.Exp, scale=0.125), reads=[ps], writes=[p_])
            for h in range(4):
                for c in range(2):
                    o = (h * 2 + c) * 16
                    mm2(h * 2 + c, 16, 130, p_[:16, o:o + 16], vb_s[0:16, b, h, :], False, [p_, vb_s])
            diff_epilogue(16, [0, 1, 2, 3], 1, lambda hh, qs: (ysb, ysb[:16, hh, :]),
                          i0=lambda hh, qs: hh * 2, i1=lambda hh, qs: hh * 2 + 1)
            pbt = pb[b % 2]
            for h in range(4):
                P.add("pe", _tp(pbt[:, h * 16:(h + 1) * 16], ysb[:16, h, :], identb[0:16, 0:16]), reads=[ysb, identb], writes=[pbt])
            P.add("act", _act(yT_s[:, 4:8, b * 16:(b + 1) * 16], pbt[:, 0:64].rearrange("p (h t) -> p h t", h=4), AF.Copy),
                  reads=[pbt], writes=[yT_s])
            P.add("pool", _dma(cka[:, :, :], cak[b, 64:576, :].rearrange("(k p) c -> p k c", p=128)), writes=[cka], dma=True)
            for kk in range(4):
                r0 = 64 + kk * 128
                P.add("pool", _dma(cva[:, kk, :, 0:64], cav[b, r0:r0 + 128, :].rearrange("p (h d) -> p h d", h=8)), writes=[cva], dma=True)
            for kk in range(4):
                pbt = pb[kk % 2]
                k_ = kts[kk % 2]
                for hp in range(4):
                    P.add("pe", _tp(pbt[:, hp * 128:(hp + 1) * 128], cka[:, kk, hp * 128:(hp + 1) * 128], identb[:, :]),
                          reads=[cka, identb], writes=[pbt])
                P.add("dve", _copy(k_[:, :, :], pbt[:, 0:512].rearrange("p (h k) -> p h k", h=4)), reads=[pbt], writes=[k_])
                ps = pf[cnt["s"] % 2]
                cnt["s"] += 1
                for h in range(8):
                    P.add("pe", _mm(ps[:, h * 16:(h + 1) * 16], k_[:, h // 2, :],
                                    qaT_s[h % 2][:, h // 2, b * 16:(b + 1) * 16], True, True), reads=[k_, qaT_s[h % 2]], writes=[ps])
                s_ = sbs[cnt["sb"] % 2]
                cnt["sb"] += 1
                P.add("dve", _stt(s_[:, 0:128].rearrange("p (h t) -> p h t", h=8), ps[:, 0:128].rearrange("p (h t) -> p h t", h=8), 0.125,
                                  gs[:, :, kk, :], ALU.mult, ALU.add), reads=[ps, gs], writes=[s_])
                p_ = pt[cnt["pt"] % 3]
                cnt["pt"] += 1
                P.add("act", _act(p_[:, 0:128], s_[:, 0:128], AF.Exp), reads=[s_], writes=[p_])
                if kk == 0:
                    opened.clear()
                for h in range(8):
                    mm2(h, 16, 66, p_[:, h * 16:(h + 1) * 16], cva[:, kk, h, :], kk == 0, [p_, cva])
            ps = pf[cnt["s"] % 2]
            cnt["s"] += 1
            for h in range(8):
                P.add("pe", _mm(ps[:16, h * 16:(h + 1) * 16], kaT_s[:, h // 2, b * 16:(b + 1) * 16],
                                qaT_s[h % 2][:, h // 2, b * 16:(b + 1) * 16], True, True), reads=[kaT_s, qaT_s[h % 2]], writes=[ps])
            s_ = sbs[cnt["sb"] % 2]
            cnt["sb"] += 1
            P.add("dve", _stt(s_[:16, 0:128].rearrange("p (h t) -> p h t", h=8), ps[:16, 0:128].rearrange("p (h t) -> p h t", h=8), 0.125,
                              gn[:, :, :], ALU.mult, ALU.add), reads=[ps, gn], writes=[s_])
            p_ = pt[cnt["pt"] % 3]
            cnt["pt"] += 1
            P.add("act", _act(p_[:16, 0:128], s_[:16, 0:128], AF.Exp), reads=[s_], writes=[p_])
            for h in range(8):
                mm2(h, 16, 66, p_[:16, h * 16:(h + 1) * 16], va_s[0:16, b, h, :], False, [p_, va_s])
            for h in range(8):
                ta, aa = acc(h, 16, 66)
                P.add("dve", lambda e, a=aa: e.reciprocal(out=ep_r[:16, 0:1], in_=a[:, 64:65]), reads=[ta], writes=[ep_r])
                P.add("dve", _ts(ysa[:16, h * 64:(h + 1) * 64], aa[:, 0:64], ep_r[:16, 0:1], None, ALU.mult), reads=[ta, ep_r], writes=[ysa])
            pbt = pb[(b + 1) % 2]
            for hp in range(4):
                P.add("pe", _tp(pbt[:, hp * 16:(hp + 1) * 16], ysa[:16, hp * 128:(hp + 1) * 128], identb[0:16, 0:16]), reads=[ysa, identb], writes=[pbt])
            P.add("act", _act(yT_s[:, 0:4, b * 16:(b + 1) * 16], pbt[:, 0:64].rearrange("p (h t) -> p h t", h=4), AF.Copy),
                  reads=[pbt], writes=[yT_s])
        if nb:
            P.add("sp", _dma(yT_scr[:, :, NPT:NPT + NS].rearrange("c p t -> p c t"), yT_s[:, :, :]), reads=[yT_s], writes=[t_yT], dma=True)

    def phase3(self, top, L):
        P = self.P
        pf, pb = self.pf, self.pb
        h_scr, g_scr, yT_scr = L["h_scr"], L["g_scr"], L["yT_scr"]
        t_h, t_g, t_yT = L["t_h"], L["t_g"], L["t_yT"]
        p3 = ExitStack()
        with p3:
            s3 = lambda name, shape, dt: self.sb(p3, name, shape, dt)
            h_t = s3("h_t3", [128, 4, D], F32)
            y_t = s3("y_t3", [128, 4, D], F32)
            xT = s3("xT3", [128, 8, 512], BF16)
            self.aT = s3("aT3", [128, 22, 512], BF16)
            yTs = [s3("yT3_0", [128, 8, 512], BF16)] * 2
            gts = [s3("gts%d" % i, [128, 2048], BF16) for i in range(2)]
            qmT = s3("qmT", [128, 8, 512], BF16)
            omT = s3("omT", [128, 8, 512], BF16)
            ptm = [s3("ptm%d" % i, [128, 512], BF16) for i in range(2)]
            t1 = s3("t1", [128, 512], F32)
            rl = t1
            t2 = s3("t2", [128, 512], F32)
            g3_ = s3("g3_", [128, D], F32)
            g4_ = s3("g4_", [128, D], F32)
            g5_ = s3("g5_", [128, D], F32)
            g7_ = s3("g7_", [128, D], F32)
            g8_ = s3("g8_", [128, D], F32)
            self.load_gain(g3_, 3, 32.0)
            self.load_gain(g4_, 4, 32.0)
            self.load_gain(g5_, 5, 32.0)
            self.load_gain(g7_, 7, 32.0)
            self.load_gain(g8_, 8, 16.0)
            mkT = s3("mkT", [128, 8, 256], BF16)
            mvb = s3("mvb", [128, 2, D], BF16)
            ckm = s3("ckm", [128, 2, D], BF16)
            ones_b = s3("ones_b", [128, 128], BF16)
            P.add("dve", _memset(ones_b[:, :], 1.0), writes=[ones_b])
            P.add("sp", _dma(mkT[:, :, :], L["mkT_scr"].rearrange("c p m -> p c m")), reads=[L["t_mkT"]], writes=[mkT], dma=True)
            P.add("sp", _dma(mvb[:, :, :], L["mv_scr"].rearrange("(mb p) d -> p mb d", p=128)), reads=[L["t_mv"]], writes=[mvb], dma=True)

            def mem_attend(c0, n):
                for hm in range(4):
                    for mb in range(2):
                        ps = pf[mb]
                        for dc in range(2):
                            P.add("pe", _mm(ps[:, 0:n], mkT[:, hm * 2 + dc, mb * 128:(mb + 1) * 128], qmT[:, hm * 2 + dc, c0:c0 + n], dc == 0, dc == 1),
                                  reads=[mkT, qmT], writes=[ps])
                        P.add("act", _act(ptm[mb][:, 0:n], ps[:, 0:n], AF.Exp, scale=1.0 / 16.0), reads=[ps], writes=[ptm[mb]])
                    for dc in range(2):
                        ps = pf[2 + dc]
                        for mb in range(2):
                            P.add("pe", _mm(ps[:, 0:n], mvb[:, mb, hm * 256 + dc * 128:hm * 256 + (dc + 1) * 128], ptm[mb][:, 0:n], mb == 0, mb == 1),
                                  reads=[mvb, ptm[mb]], writes=[ps])
                    ps = pf[4]
                    for mb in range(2):
                        P.add("pe", _mm(ps[:, 0:n], ones_b[:, :], ptm[mb][:, 0:n], mb == 0, mb == 1), reads=[ones_b, ptm[mb]], writes=[ps])
                    P.add("dve", lambda e, ps=ps: e.reciprocal(out=rl[:, 0:n], in_=ps[:, 0:n]), reads=[ps], writes=[rl])
                    for dc in range(2):
                        P.add("dve", _tt(omT[:, hm * 2 + dc, c0:c0 + n], pf[2 + dc][:, 0:n], rl[:, 0:n], ALU.mult),
                              reads=[pf[2 + dc], rl], writes=[omT])

            tiles = [(i, 512) for i in range(8)] + [(8, NS)]
            if os.environ.get("K_TILES3"):
                tiles = [tiles[int(i)] for i in os.environ["K_TILES3"].split(",") if i != "x"]
            def load_yT(idx3):
                ti_, ntok_ = tiles[idx3]
                P.add("sp", _dma(yTs[idx3 % 2][:, :, 0:ntok_], yT_scr[:, :, ti_ * 512:ti_ * 512 + ntok_].rearrange("c p t -> p c t")),
                      reads=[t_yT], writes=[yTs[idx3 % 2]], dma=True)

            for idx3, (ti, ntok) in enumerate(tiles):
                samp = ti == 8
                rows = min(128, ntok)
                TB = max(1, ntok // 128)
                tok0 = ti * 512
                yT = yTs[idx3 % 2]
                load_yT(idx3)
                P.add("sp", _dma(h_t[:rows, 0:TB, :], h_scr[tok0:tok0 + ntok, :].rearrange("(tb p) d -> p tb d", p=rows)),
                      reads=[t_h], writes=[h_t], dma=True)
                wb = self.wbuf[self.wcnt % 2]
                self.wcnt += 1
                wv = wb[:, 0:8192].rearrange("p (k c) -> p k c", k=8)
                self.load_w(wb, wv[:, 0:4, :], L["w_br_a"], 4)
                self.load_w(wb, wv[:, 4:8, :], L["w_br_b"], 4)
                for tb in range(TB):
                    g_ = gts[tb % 2]
                    r0 = tok0 + tb * rows
                    P.add("sp", _dma(g_[:rows, :], g_scr[r0:r0 + rows, :]), reads=[t_g], writes=[g_], dma=True)
                    mg = self.xn[tb % 2]
                    for half in range(2):
                        psA = pf[2 * half]
                        psB = pf[2 * half + 1]
                        for kc in range(4):
                            P.add("pe", _mm(psA[:rows, :], yT[:, kc, tb * rows:(tb + 1) * rows], wv[:, kc, half * 512:(half + 1) * 512], kc == 0, kc == 3),
                                  reads=[yT, wb], writes=[psA])
                        for kc in range(4):
                            P.add("pe", _mm(psB[:rows, :], yT[:, 4 + kc, tb * rows:(tb + 1) * rows], wv[:, 4 + kc, half * 512:(half + 1) * 512], kc == 0, kc == 3),
                                  reads=[yT, wb], writes=[psB])
                        P.add("dve", _tt(t1[:rows, :], psA[:rows, :], g_[:rows, half * 512:(half + 1) * 512], ALU.mult), reads=[psA, g_], writes=[t1])
                        P.add("dve", _tt(t2[:rows, :], psB[:rows, :], g_[:rows, 1024 + half * 512:1024 + (half + 1) * 512], ALU.mult),
                              reads=[psB, g_], writes=[t2])
                        P.add("dve", _tt(mg[:rows, half * 512:(half + 1) * 512], t1[:rows, :], t2[:rows, :], ALU.add), reads=[t1, t2], writes=[mg])
                    pbt = pb[tb % 2]
                    for kc in range(8):
                        P.add("pe", _tp(pbt[:, kc * 128:kc * 128 + rows], mg[:rows, kc * 128:(kc + 1) * 128], self.identb[:rows, :rows]),
                              reads=[mg, self.identb], writes=[pbt])
                    P.add("act", _act(xT[:, :, tb * rows:(tb + 1) * rows], pbt[:, 0:1024].rearrange("p (k c) -> p k c", k=8)[:, :, 0:rows], AF.Copy),
                          reads=[pbt], writes=[xT])
                self.linear_tm(xT, rows, TB, L["w_out"], y_t)
                self.resid_norm(h_t, y_t, rows, TB, g3_)
                self.rms_T(h_t, rows, TB, g4_, xT)
                wb = self.wbuf[self.wcnt % 2]
                self.wcnt += 1
                wv = wb[:, 0:8192].rearrange("p (k c) -> p k c", k=8)
                self.load_w(wb, wv, L["w_mq"], 8)
                for c in range(8):
                    ps = pf[c % 4]
                    for kc in range(8):
                        P.add("pe", _mm(ps[:, 0:ntok], wv[:, kc, c * 128:(c + 1) * 128], xT[:, kc, 0:ntok], kc == 0, kc == 7), reads=[wb, xT], writes=[ps])
                    if c % 2 == 0:
                        P.add("act", _act(qmT[:, c, 0:ntok], ps[:, 0:ntok], AF.Copy), reads=[ps], writes=[qmT])
                    else:
                        P.add("dve", _copy(qmT[:, c, 0:ntok], ps[:, 0:ntok]), reads=[ps], writes=[qmT])
                if not samp:
                    mem_attend(0, ntok)
                else:
                    for b in range(4):
                        P.add("pool", _dma(ckm[:, :, :], L["cmk"][b, :, :].rearrange("(mb p) d -> p mb d", p=128)), writes=[ckm], dma=True)
                        P.add("pool", _dma(mvb[:, :, :], L["cmv"][b, :, :].rearrange("(mb p) d -> p mb d", p=128)), writes=[mvb], dma=True)
                        for mb in range(2):
                            pbt = pb[mb]
                            for c in range(8):
                                P.add("pe", _tp(pbt[:, c * 128:(c + 1) * 128], ckm[:, mb, c * 128:(c + 1) * 128], self.identb[:, :]),
                                      reads=[ckm, self.identb], writes=[pbt])
                            P.add("dve", _copy(mkT[:, :, mb * 128:(mb + 1) * 128], pbt[:, 0:1024].rearrange("p (c m) -> p c m", c=8)),
                                  reads=[pbt], writes=[mkT])
                        mem_attend(b * 16, 16)
                self.linear_tm(omT, rows, TB, L["w_mo"], y_t)
                self.resid_norm(h_t, y_t, rows, TB, g5_)
                self.rms_T(h_t, rows, TB, g7_, xT)
                self.swiglu_ffn(xT, ntok, rows, TB, L["f2u"], L["f2d"], y_t)
                self.resid_norm(h_t, y_t, rows, TB, g8_)
                outt = L["y_s"] if samp else L["y_p"][tok0:tok0 + ntok, :]
                P.add("sp", _dma(outt.rearrange("(tb p) d -> p tb d", p=rows), h_t[:rows, 0:TB, :]), reads=[h_t], dma=True, out=True)

    def linear_tm(self, xT, rows, TB, w, y_t):
        P = self.P
        wb = self.wbuf[self.wcnt % 2]
        self.wcnt += 1
        wv = wb[:, 0:8192].rearrange("p (k c) -> p k c", k=8)
        self.load_w(wb, wv, w, 8)
        for tb in range(TB):
            for half in range(2):
                ps = self.pf[4 + half]
                for kc in range(8):
                    P.add("pe", _mm(ps[:rows, :], xT[:, kc, tb * rows:(tb + 1) * rows], wv[:, kc, half * 512:(half + 1) * 512], kc == 0, kc == 7),
                          reads=[xT, wb], writes=[ps])
                P.add("act", _act(y_t[:rows, tb, half * 512:(half + 1) * 512], ps[:rows, :], AF.Copy), reads=[ps], writes=[y_t])


    def build(self):
        nc = self.nc
        P = self.P
        xp = self.din("xp", [NPT, D])
        xs = self.din("xs", [NS, D])
        cak = self.din("cak", [4, 576, 512])
        cav = self.din("cav", [4, 576, 512])
        cbk = self.din("cbk", [4, 4096, 512])
        cbv = self.din("cbv", [4, 4096, 512])
        cmk = self.din("cmk", [4, 256, 1024])
        cmv = self.din("cmv", [4, 256, 1024])
        memp = self.din("memp", [256, D])
        w_in = self.din("w_in", [D, 3072])
        w_gate = self.din("w_gate", [D, 2048])
        b_gate = self.din("b_gate", [1, 2048])
        rel_bias = self.din("rel_bias", [8, 257])
        lam_qk = self.din("lam_qk", [1, 256])
        subln_g = self.din("subln_g", [1, 128])
        w_br_a = self.din("w_br_a", [512, D])
        w_br_b = self.din("w_br_b", [512, D])
        w_out = self.din("w_out", [D, D])
        w_mq = self.din("w_mq", [D, D])
        w_mkv = self.din("w_mkv", [D, 2048])
        w_mo = self.din("w_mo", [D, D])
        self.norm_g = self.din("norm_g", [9, D])
        f1u = self.din("f1u", [D, 2 * FF])
        f1d = self.din("f1d", [FF, D])
        f2u = self.din("f2u", [D, 2 * FF])
        f2d = self.din("f2d", [FF, D])
        self.wf32 = {}
        self.wtl = {}
        Lw = locals()
        twins = {}
        for nm in ["w_in", "w_gate", "w_br_a", "w_br_b", "w_out", "w_mq", "w_mkv", "w_mo", "f1u", "f1d", "f2u", "f2d"]:
            f32ap = Lw[nm]
            bfap = self.dint(nm + "_bf", list(f32ap.shape), BF16)
            self.wf32[nm] = (f32ap, bfap)
            self.wtl[bfap.tensor.name] = []
            twins[nm] = bfap
        w_in, w_gate, w_br_a, w_br_b = twins["w_in"], twins["w_gate"], twins["w_br_a"], twins["w_br_b"]
        w_out, w_mq, w_mkv, w_mo = twins["w_out"], twins["w_mq"], twins["w_mkv"], twins["w_mo"]
        f1u, f1d, f2u, f2d = twins["f1u"], twins["f1d"], twins["f2u"], twins["f2d"]
        c_identb = self.din("c_identb", [128, 128], BF16)
        c_identf = self.din("c_identf", [128, 128])
        c_jf = self.din("c_jf", [128, 128])
        c_btab = self.din("c_btab", [128, 4 * 128])
        c_dtab = self.din("c_dtab", [128, 4 * 8 * 256], BF16)
        c_btabs = self.din("c_btabs", [128, 4 * 33])
        c_dtabs = self.din("c_dtabs", [16, 4 * 16], BF16)
        c_maska = self.din("c_maska", [128, 12 * 256])
        c_gb = self.din("c_gb", [4, 128, 12 * 2 * 256])
        c_gs = self.din("c_gs", [128, 8 * 4 * 16])
        c_gn = self.din("c_gn", [16, 8 * 16])

        y_p = self.dout("y_p", [NPT, D])
        y_s = self.dout("y_s", [NS, D])
        ak_last = self.dout("ak_last", [256, 512])
        av_last = self.dout("av_last", [256, 512])
        bk_p = self.dout("bk_p", [NPT, 512])
        bv_p = self.dout("bv_p", [NPT, 512])
        mk_p = self.dout("mk_p", [256, D])
        mv_p = self.dout("mv_p", [256, D])
        ak_s = self.dout("ak_s", [4, 576, 512])
        av_s = self.dout("av_s", [4, 576, 512])
        bk_s = self.dout("bk_s", [NS, 512])
        bv_s = self.dout("bv_s", [NS, 512])

        h_scr = self.dint("h_scr", [NTOK, D], F32)
        g_scr = self.dint("g_scr", [NTOK, 2048], BF16)
        qa_scr = self.dint("qa_scr", [4, 128, NPT], BF16)
        qb_scr = self.dint("qb_scr", [4, 128, NPT], BF16)
        yT_scr = self.dint("yT_scr", [8, 128, NTOK], BF16)
        kaT_in = self.dint("kaT_in", [32 * 128, 512], BF16)
        kbT_in = self.dint("kbT_in", [32 * 128, 512], BF16)
        va_in = self.dint("va_in", [NPT, 512], BF16)
        vb_in = self.dint("vb_in", [NPT, 512], BF16)
        kaT_g = self.dint("kaT_g", [4 * 32 * 128, 512], BF16)
        kbT_g = self.dint("kbT_g", [4 * 32 * 128, 512], BF16)
        va_g = self.dint("va_g", [4 * NPT, 512], BF16)
        vb_g = self.dint("vb_g", [4 * NPT, 512], BF16)
        mkT_scr = self.dint("mkT_scr", [8, 128, 256], BF16)
        mv_scr = self.dint("mv_scr", [256, D], BF16)
        t_mkT = Tl(None, "mkT_scr")
        t_mv = Tl(None, "mv_scr")
        t_h = Tl(None, "h_scr")
        t_g = Tl(None, "g_scr")
        t_qa = Tl(None, "qa_scr")
        t_qb = Tl(None, "qb_scr")
        t_yT = Tl(None, "yT_scr")
        t_kaT_in = [[] for i in range(4)]
        t_kbT_in = [[] for i in range(4)]
        t_va_in = [[] for i in range(4)]
        t_vb_in = [[] for i in range(4)]
        t_kaT_g = [Tl(None, "kaT_g%d" % i) for i in range(4)]
        t_kbT_g = [Tl(None, "kbT_g%d" % i) for i in range(4)]
        t_va_g = [Tl(None, "va_g%d" % i) for i in range(4)]
        t_vb_g = [Tl(None, "vb_g%d" % i) for i in range(4)]

        top = ExitStack()
        with top:
            sb = lambda name, shape, dt: self.sb(top, name, shape, dt)
            self.identb = sb("identb", [128, 128], BF16)
            self.identf = sb("identf", [128, 128], F32)
            self.jf = sb("jf", [128, 128], F32)
            self.ss = sb("ss", [128, 8], F32)
            self.rs = sb("rs", [128, 8], F32)
            self.junk = sb("junk", [128, 1024], BF16)
            self.xn = [sb("xn%d" % i, [128, 1024], BF16) for i in range(2)]
            self.sg = [sb("sg%d" % i, [128, 512], F32) for i in range(2)]
            self.wbuf = [sb("wbuf%d" % i, [128, 12288], BF16) for i in range(2)]
            self.wcnt = 0
            ones2 = sb("ones2", [2, 128], BF16)
            bar_t = {e: sb("bar_" + e, [128, 8], F32) for e in ENGS}
            bar_t2 = {e: sb("bar2_" + e, [128, 8], F32) for e in ENGS}
            bar_src = sb("bar_src", [128, 8], F32)
            qaT_s = [sb("qaT_s%d" % i, [128, 4, NS], BF16) for i in range(2)]
            kaT_s = sb("kaT_s", [128, 4, NS], BF16)
            qbT_s = [sb("qbT_s%d" % i, [128, 4, NS], BF16) for i in range(2)]
            kbT_s = sb("kbT_s", [128, 4, NS], BF16)
            va_s = sb("va_s", [16, 4, 8, 66], BF16)
            vb_s = sb("vb_s", [16, 4, 4, 130], BF16)
            self.pf = [Tl(top.enter_context(nc.psum_tensor("pf%d" % i, [128, 512], F32)), "pf%d" % i) for i in range(6)]
            self.pb = [Tl(top.enter_context(nc.psum_tensor("pb%d" % i, [128, 1024], BF16)), "pb%d" % i) for i in range(2)]

            P.add("sp", _dma(self.identb[:, :], c_identb[:, :]), writes=[self.identb], dma=True)
            P.add("sp", _dma(self.identf[:, :], c_identf[:, :]), writes=[self.identf], dma=True)
            P.add("sp", _dma(self.jf[:, :], c_jf[:, :]), writes=[self.jf], dma=True)
            P.add("dve", _memset(ones2[:, :], 1.0), writes=[ones2])
            P.add("dve", _memset(bar_src[:, :], 0.0), writes=[bar_src])
            for t_ in qaT_s + qbT_s:
                P.add("dve", _memset(t_[:, :, :], 0.0), writes=[t_])
            P.add("dve", _memset(va_s[:, :, :, 64:66], 1.0), writes=[va_s])
            P.add("dve", _memset(vb_s[:, :, :, 128:130], 1.0), writes=[vb_s])

            def mkbar(e, ph):
                t = bar_t[e] if ph == 0 else bar_t2[e]
                if e == "pe":
                    return t, _mm(self.pf[5][0:8, 0:8], self.identb[0:8, 0:8], self.identb[0:8, 0:8], True, True)
                if e == "sp":
                    return t, _dma(t[:, :], bar_src[:, :])
                if e == "act":
                    return t, _act(t[:, :], bar_src[:, :], AF.Copy)
                return t, _copy(t[:, :], bar_src[:, :])

            self.mkbar = mkbar
            for nm in ["w_mkv", "f1u", "f1d", "w_in", "w_gate"]:
                self.convert_w(nm)
            p1 = ExitStack()
            with p1:
                s1 = lambda name, shape, dt: self.sb(p1, name, shape, dt)
                x_t = s1("x_t", [128, 4, D], F32)
                x_t2 = s1("x_t2", [128, 4, D], F32)
                x_ts = [x_t, x_t2]
                y_t = s1("y_t", [128, 4, D], F32)
                xT = s1("xT", [128, 8, 512], BF16)
                self.aT = s1("aT", [128, 22, 512], BF16)
                g0 = s1("g0", [128, D], F32)
                g1 = s1("g1", [128, D], F32)
                g2 = s1("g2", [128, D], F32)
                qst = [s1("qst%d" % i, [128, 4, 512], BF16) for i in range(2)]
                kvst = [s1("kvst0", [128, 4, 512], F32)] * 2
                vst = [s1("vst0", [128, 4, 512], BF16)] * 2
                gst = [s1("gst%d" % i, [128, 1536], BF16) for i in range(2)]
                bg2 = s1("bg2", [2, 2048], BF16)
                self.load_gain(g0, 0, 32.0)
                self.load_gain(g2, 2, 32.0)
                self.load_gain(g1, 6, 32.0)
                bg_f = y_t[0:2, 0:2, :].rearrange("p a b -> p (a b)")
                bg_h = vst[0][0:2, :, :].rearrange("p a b -> p (a b)")
                P.add("sp", _dma(bg_f[0:1, :], b_gate[0:1, :]), writes=[y_t], dma=True)
                P.add("sp", _dma(bg_f[1:2, :], b_gate[0:1, :]), writes=[y_t], dma=True)
                P.add("dve", _copy(bg_h, bg_f), reads=[y_t], writes=[vst[0]])
                P.add("dve", _copy(bg2[:, :], bg_h), reads=[vst[0]], writes=[bg2])
                P.add("dve", _tt(bg_f, bg_f, bg_h, ALU.subtract), reads=[y_t, vst[0]], writes=[y_t])
                P.add("dve", _copy(bg_h, bg_f), reads=[y_t], writes=[vst[0]])
                P.add("sp", _dma(bg2[1:2, :], bg_h[1:2, :]), reads=[vst[0]], writes=[bg2], dma=True)

                if not os.environ.get("K_NOMEM"):
                    self.mem_kv(memp, g1, w_mkv, mk_p, mv_p, x_t, xT, kvst[0], qst[0], mkT_scr, t_mkT, mv_scr, t_mv)
                self.load_gain(g1, 1, 16.0)

                for b in range(4 if not os.environ.get("K_NOSHIFT") else 0):
                    P.add("sp", _dma(ak_s[b, 0:560, :], cak[b, 16:576, :]), dma=True, out=True)
                    P.add("sp", _dma(av_s[b, 0:560, :], cav[b, 16:576, :]), dma=True, out=True)
                tiles = [(i, 512, xp[i * 512:(i + 1) * 512, :]) for i in range(8)] + [(8, NS, xs[:, :])]
                if os.environ.get("K_TILES"):
                    tiles = [tiles[int(i)] for i in os.environ["K_TILES"].split(",") if i != "x"]
                kcnt = 0
                def load_x(idx1):
                    ti_, ntok_, src_ = tiles[idx1]
                    rows_ = min(128, ntok_)
                    TB_ = max(1, ntok_ // 128)
                    xt_ = x_ts[idx1 % 2]
                    P.add("sp", _dma(xt_[:rows_, 0:TB_, :], src_.rearrange("(tb p) d -> p tb d", p=rows_)), writes=[xt_], dma=True)

                if tiles:
                    load_x(0)
                for idx1, (ti, ntok, src) in enumerate(tiles):
                    samp = ti == 8
                    rows = min(128, ntok)
                    TB = max(1, ntok // 128)
                    tok0 = ti * 512
                    x_t = x_ts[idx1 % 2]
                    if idx1 + 1 < len(tiles):
                        load_x(idx1 + 1)
                    PARTS = os.environ.get("K_PARTS", "ffn,res,win,gates").split(",")
                    self.rms_T(x_t, rows, TB, g0, xT)
                    if "ffn" in PARTS:
                        self.swiglu_ffn(xT, ntok, rows, TB, f1u, f1d, y_t)
                    if "res" in PARTS:
                        self.resid_norm(x_t, y_t, rows, TB, g1)
                    P.add("sp", _dma(h_scr[tok0:tok0 + ntok, :].rearrange("(tb p) d -> p tb d", p=rows),
                                     x_t[:rows, 0:TB, :]), reads=[x_t], writes=[Tl(None, 'w_h')], dma=True)
                    self.rms_T(x_t, rows, TB, g2, xT)
                    for grp in range(2 if "win" in PARTS else 0):
                        wb = self.wbuf[self.wcnt % 2]
                        self.wcnt += 1
                        wv = wb[:, 0:12288].rearrange("p (k c) -> p k c", k=8)
                        self.load_w(wb, wv, w_in[:, grp * 1536:(grp + 1) * 1536], 8)
                        WIN = os.environ.get("K_WIN", "fm,tm,fmd,tmd").split(",")
                        for which in range(2 if "fm" in WIN else 0):
                            st_ = qst[kcnt % 2]
                            kcnt += 1
                            for c in range(4):
                                ps = self.pf[c % 4]
                                cc0 = (which * 4 + c) * 128
                                for kc in range(8):
                                    P.add("pe", _mm(ps[:, 0:ntok], wv[:, kc, cc0:cc0 + 128], xT[:, kc, 0:ntok], kc == 0, kc == 7),
                                          reads=[wb, xT], writes=[ps])
                                if samp and which == 0:
                                    dq = qaT_s if grp == 0 else qbT_s
                                    P.add("act", _act(dq[0][0:64, c, :], ps[0:64, 0:ntok], AF.Copy), reads=[ps], writes=[dq[0]])
                                    P.add("act", _act(dq[1][64:128, c, :], ps[64:128, 0:ntok], AF.Copy), reads=[ps], writes=[dq[1]])
                                elif samp:
                                    dst = kaT_s if grp == 0 else kbT_s
                                    P.add("act", _act(dst[:, c, :], ps[:, 0:ntok], AF.Copy), reads=[ps], writes=[dst])
                                elif c % 2 == 0:
                                    P.add("act", _act(st_[:, c, :], ps[:, :], AF.Copy), reads=[ps], writes=[st_])
                                else:
                                    P.add("dve", _copy(st_[:, c, :], ps[:, :]), reads=[ps], writes=[st_])
                            if samp or "fmd" not in WIN:
                                continue
                            if which == 0:
                                dscr, dtl = (qa_scr, t_qa) if grp == 0 else (qb_scr, t_qb)
                                P.add("sp", _dma(dscr[:, :, tok0:tok0 + 512].rearrange("c p t -> p c t"), st_[:, :, :]),
                                      reads=[st_], writes=[Tl(None, 'w_q')], dma=True)
                            else:
                                dscr, dtl = (kaT_in, t_kaT_in[ti // 2]) if grp == 0 else (kbT_in, t_kbT_in[ti // 2])
                                for b in range(4):
                                    b0 = ti * 4 + b
                                    dst = dscr[b0 * 128:(b0 + 1) * 128, :].rearrange("p (c k) -> p c k", c=4)
                                    wt = Tl(None, "w_kT")
                                    dtl.append(wt)
                                    P.add("sp", _dma(dst, st_[:, :, b * 128:(b + 1) * 128]),
                                          reads=[st_], writes=[wt], dma=True)
                        for which in range(2 if "tm" in WIN else 0):
                            kv = kvst[kcnt % 2]
                            vb_ = vst[kcnt % 2]
                            kcnt += 1
                            cc0 = 512 + which * 512
                            for tb in range(TB):
                                ps = self.pf[4 + tb % 2]
                                for kc in range(8):
                                    P.add("pe", _mm(ps[:rows, :], xT[:, kc, tb * rows:(tb + 1) * rows], wv[:, kc, cc0:cc0 + 512],
                                                    kc == 0, kc == 7), reads=[wb, xT], writes=[ps])
                                P.add("dve", _copy(kv[:rows, tb, :], ps[:rows, :]), reads=[ps], writes=[kv])
                                if which == 1 and not samp:
                                    P.add("act", _act(vb_[:rows, tb, :], kv[:rows, tb, :], AF.Copy), reads=[kv], writes=[vb_])
                            if samp:
                                outt = [[ak_s, av_s], [bk_s, bv_s]][grp][which]
                                if grp == 0:
                                    for b in range(4):
                                        P.add("sp", _dma(outt[b, 560:576, :], kv[b * 16:(b + 1) * 16, 0, :]), reads=[kv], dma=True, out=True)
                                else:
                                    P.add("sp", _dma(outt[:, :], kv[:NS, 0, :]), reads=[kv], dma=True, out=True)
                                if which == 1:
                                    for b in range(4):
                                        ps = self.pf[4 + b % 2]
                                        for kc in range(8):
                                            P.add("pe", _mm(ps[:16, :], xT[:, kc, b * 16:(b + 1) * 16], wv[:, kc, cc0:cc0 + 512],
                                                            kc == 0, kc == 7), reads=[wb, xT], writes=[ps])
                                        if grp == 0:
                                            P.add("dve", _copy(va_s[:, b, :, 0:64], ps[:16, :].rearrange("p (h d) -> p h d", h=8)),
                                                  reads=[ps], writes=[va_s])
                                        else:
                                            P.add("dve", _copy(vb_s[:, b, :, 0:128], ps[:16, :].rearrange("p (h d) -> p h d", h=4)),
                                                  reads=[ps], writes=[vb_s])
                                continue
                            if "tmd" not in WIN:
                                continue
                            if grp == 1:
                                outt = bk_p if which == 0 else bv_p
                                P.add("sp", _dma(outt[tok0:tok0 + 512, :].rearrange("(tb p) d -> p tb d", p=128), kv[:, :, :]),
                                      reads=[kv], dma=True, out=True)
                            elif ti == 7:
                                outt = ak_last if which == 0 else av_last
                                P.add("sp", _dma(outt[:, :].rearrange("(tb p) d -> p tb d", p=128), kv[:, 2:4, :]),
                                      reads=[kv], dma=True, out=True)
                            if which == 1:
                                dscr, dtl = (va_in, t_va_in[ti // 2]) if grp == 0 else (vb_in, t_vb_in[ti // 2])
                                wt = Tl(None, "w_v")
                                dtl.append(wt)
                                P.add("sp", _dma(dscr[tok0:tok0 + 512, :].rearrange("(tb p) d -> p tb d", p=128), vb_[:, :, :]),
                                      reads=[vb_], writes=[wt], dma=True)
                    gcnt = 0
                    for grp in range(2 if "gates" in PARTS else 0):
                        wb = self.wbuf[self.wcnt % 2]
                        self.wcnt += 1
                        ncol = 1536 if grp == 0 else 512
                        wv = wb[:, 0:12288].rearrange("p (k c) -> p k c", k=8)
                        self.load_w(wb, wv[:, :, 0:ncol], w_gate[:, grp * 1536:grp * 1536 + ncol], 8)
                        for tb in range(TB):
                            gs = gst[gcnt % 2]
                            gcnt += 1
                            for pc in range(ncol // 512):
                                ps = self.pf[(tb * 3 + pc) % 4]
                                gc0 = grp * 1536 + pc * 512
                                for kc in range(8):
                                    P.add("pe", _mm(ps[:rows, :], xT[:, kc, tb * rows:(tb + 1) * rows], wv[:, kc, pc * 512:(pc + 1) * 512],
                                                    kc == 0, False), reads=[wb, xT], writes=[ps])
                                P.add("pe", _mm(ps[:rows, :], ones2[:, 0:rows], bg2[:, gc0:gc0 + 512], False, True),
                                      reads=[ones2, bg2], writes=[ps])
                                P.add("act", _act(gs[:rows, pc * 512:(pc + 1) * 512], ps[:rows, :], AF.Sigmoid), reads=[ps], writes=[gs])
                            r0 = tok0 + tb * rows
                            P.add("sp", _dma(g_scr[r0:r0 + rows, grp * 1536:grp * 1536 + ncol], gs[:rows, 0:ncol]),
                                  reads=[gs], writes=[Tl(None, 'w_g')], dma=True)
                    if (not samp) and ti % 2 == 1:
                        self.exchange(ti // 2, locals())
                P.barrier(mkbar, {'pe': [self.pf[5]]})
            self.phase23(top, locals())
            P.emit(nc, top)
        return nc


_NC = None


def _get_nc():
    global _NC
    if _NC is None:
        _NC = K().build()
    return _NC


def _consts(r):
    bf = ml_dtypes.bfloat16
    c = {}
    c["c_identb"] = np.eye(128, dtype=np.float32).astype(bf)
    c["c_identf"] = np.eye(128, dtype=np.float32)
    c["c_jf"] = np.ascontiguousarray(np.eye(128, dtype=np.float32)[:, ::-1])
    p = np.arange(128, dtype=np.float64)[:, None, None]
    sl = np.array(SLOPES, dtype=np.float64)[None, :, None]
    oi = np.arange(128, dtype=np.float64)[None, None, :]
    c["c_btab"] = (sl * (128.0 * (oi - 120.0) + p - 256.0 * (r + 1))).astype(np.float32).reshape(128, 512)
    i = np.arange(8)[None, None, :, None]
    q = np.arange(256)[None, None, None, :]
    pk = 128 * i + np.arange(128)[:, None, None, None]
    pq = 256 * r + q
    sl4 = np.array(SLOPES, dtype=np.float64)[None, :, None, None]
    d8 = -16.0 * sl4 * np.maximum(pk - pq, 0)
    d8 = np.where((pk // 64) > (pq // 64), NEG, d8)
    c["c_dtab"] = d8.astype(np.float32).astype(bf).reshape(128, 4 * 8 * 256)
    kb = np.arange(33, dtype=np.float64)[None, None, :]
    bs = sl * (128.0 * kb + p - 4096.0)
    bs[:, :, 32] = 0.0
    c["c_btabs"] = bs.astype(np.float32).reshape(128, 4 * 33)
    tk = np.arange(16)[:, None, None]
    tq = np.arange(16)[None, None, :]
    ds_ = 8.0 * np.array(SLOPES)[None, :, None] * (tq - np.abs(tq - tk))
    c["c_dtabs"] = ds_.astype(np.float32).astype(bf).reshape(16, 64)
    ia = np.arange(12)[None, :, None]
    pka = 128 * (ia - 4) + np.arange(128)[:, None, None]
    pqa = 256 * r + np.arange(256)[None, None, :]
    ck = np.floor_divide(pka, 64)
    cq = pqa // 64
    ok = (ck <= cq) & (ck >= cq - 8)
    c["c_maska"] = np.where(ok, 0.0, NEG).astype(np.float32).reshape(128, 12 * 256)
    return c


def _relbias_tables(rb, r):
    p = np.arange(128)
    i = np.arange(12)[None, :, None]
    q = np.arange(256)[None, None, :]
    delta = q - p[:, None, None] + 128 * (2 * r + 4 - i)
    idx = np.clip(delta, -128, 128) + 128
    g = rb[:, idx]
    gb = g.reshape(4, 2, 128, 12, 256).transpose(0, 2, 3, 1, 4)
    out = {"c_gb": np.ascontiguousarray(gb.reshape(4, 128, 12 * 2 * 256).astype(np.float32))}
    blk = np.arange(4)[None, :, None]
    t = np.arange(16)[None, None, :]
    ds_ = 512 + t - 128 * blk - p[:, None, None]
    gs = rb[:, np.clip(ds_, -128, 128) + 128]
    out["c_gs"] = np.ascontiguousarray(gs.transpose(1, 0, 2, 3).reshape(128, 8 * 4 * 16).astype(np.float32))
    tk = np.arange(16)[:, None]
    tq = np.arange(16)[None, :]
    gn = rb[:, np.clip(tq - tk, -128, 128) + 128]
    out["c_gn"] = np.ascontiguousarray(gn.transpose(1, 0, 2).reshape(16, 128).astype(np.float32))
    return out


def _stripe(a, r):
    sh = a.shape
    return np.ascontiguousarray(a.reshape((64, 256) + sh[1:])[r::4].reshape((4096,) + sh[1:]))


def kernel(x_prompt, x_sample, cache_a_k, cache_a_v, cache_b_k, cache_b_v, cache_mem_k, cache_mem_v,
           mem_prompt, w_in, w_gate, b_gate, rel_bias, lam_qk, subln_g, w_br_a, w_br_b, w_out,
           w_mq, w_mkv, w_mo, norm_g, ffn1_up, ffn1_down, ffn2_up, ffn2_down):
    f = lambda a: np.ascontiguousarray(np.asarray(a, dtype=np.float32))
    nc = _get_nc()
    shared = {
        "w_in": f(w_in[0]), "w_gate": f(w_gate[0]), "b_gate": f(b_gate[0]).reshape(1, 2048),
        "rel_bias": f(rel_bias[0]), "lam_qk": f(lam_qk[0]).reshape(1, 256), "subln_g": f(subln_g[0]).reshape(1, 128),
        "w_br_a": f(w_br_a[0]), "w_br_b": f(w_br_b[0]), "w_out": f(w_out[0]), "w_mq": f(w_mq[0]),
        "w_mkv": f(w_mkv[0]), "w_mo": f(w_mo[0]), "norm_g": f(norm_g[0]),
        "f1u": f(ffn1_up[0]), "f1d": f(ffn1_down[0]), "f2u": f(ffn2_up[0]), "f2d": f(ffn2_down[0]),
    }
    xpn = f(x_prompt)
    xsn = f(x_sample)
    in_maps = []
    for c in range(8):
        n, r = divmod(c, 4)
        m = dict(shared)
        m["xp"] = _stripe(xpn[n], r)
        m["xs"] = xsn[4 * c:4 * c + 4].reshape(NS, D)
        m["cak"] = f(cache_a_k[0, 4 * c:4 * c + 4]).reshape(4, 576, 512)
        m["cav"] = f(cache_a_v[0, 4 * c:4 * c + 4]).reshape(4, 576, 512)
        m["cbk"] = f(cache_b_k[0, 4 * c:4 * c + 4]).reshape(4, 4096, 512)
        m["cbv"] = f(cache_b_v[0, 4 * c:4 * c + 4]).reshape(4, 4096, 512)
        m["cmk"] = f(cache_mem_k[0, 4 * c:4 * c + 4]).reshape(4, 256, 1024)
        m["cmv"] = f(cache_mem_v[0, 4 * c:4 * c + 4]).reshape(4, 256, 1024)
        m["memp"] = f(mem_prompt[n])
        m.update(_consts(r))
        m.update(_relbias_tables(shared["rel_bias"], r))
        in_maps.append(m)
    ncr = int(os.environ.get("K_NCORES", "8"))
    if ncr < 8:
        return run_bass_kernel_spmd(nc, in_maps[:ncr], core_ids=list(range(ncr))).results
    res = run_bass_kernel_spmd(nc, in_maps, core_ids=list(range(8))).results
    g = lambda c, k: np.asarray(res[c][k], dtype=np.float32)
    y_prompt = np.zeros((2, 16384, D), np.float32)
    bkp = np.zeros((2, 16384, 512), np.float32)
    bvp = np.zeros((2, 16384, 512), np.float32)
    for c in range(8):
        n, r = divmod(c, 4)
        y_prompt[n].reshape(64, 256, D)[r::4] = g(c, "y_p").reshape(16, 256, D)
        bkp[n].reshape(64, 256, 512)[r::4] = g(c, "bk_p").reshape(16, 256, 512)
        bvp[n].reshape(64, 256, 512)[r::4] = g(c, "bv_p").reshape(16, 256, 512)
    y_sample = np.concatenate([g(c, "y_s").reshape(4, 16, D) for c in range(8)], 0)
    akp = np.stack([np.concatenate([g(4 * n + 1, "ak_last")[192:256], g(4 * n + 2, "ak_last"), g(4 * n + 3, "ak_last")], 0)
                    for n in range(2)], 0)
    avp = np.stack([np.concatenate([g(4 * n + 1, "av_last")[192:256], g(4 * n + 2, "av_last"), g(4 * n + 3, "av_last")], 0)
                    for n in range(2)], 0)
    mkp = np.stack([g(4 * n, "mk_p") for n in range(2)], 0)
    mvp = np.stack([g(4 * n, "mv_p") for n in range(2)], 0)
    aks = np.concatenate([g(c, "ak_s") for c in range(8)], 0)
    avs = np.concatenate([g(c, "av_s") for c in range(8)], 0)
    bks = np.concatenate([g(c, "bk_s").reshape(4, 16, 512) for c in range(8)], 0)
    bvs = np.concatenate([g(c, "bv_s").reshape(4, 16, 512) for c in range(8)], 0)
    return (y_prompt, y_sample,
            akp.reshape(1, 2, 576, 8, 64), avp.reshape(1, 2, 576, 8, 64),
            bkp.reshape(1, 2, 16384, 4, 2, 64), bvp.reshape(1, 2, 16384, 4, 128),
            mkp.reshape(1, 2, 256, 4, 256), mvp.reshape(1, 2, 256, 4, 256),
            aks.reshape(1, 32, 576, 8, 64), avs.reshape(1, 32, 576, 8, 64),
            bks.reshape(1, 32, 16, 4, 2, 64), bvs.reshape(1, 32, 16, 4, 128))
```
